# Optimizing a Trainium2 kernel written in Bass

```python
import jax, jax.numpy as jnp
from jax import lax
import numpy as np

D_MODEL = 4096
BATCH = 1
SEQ = 16384
DEPTH = 4

CHUNK = 64
GMLP_BLOCK = 128
GMLP_WIDTH = D_MODEL // 2
GMLP_GROUP_DIM = 128
GMLP_GROUPS = GMLP_WIDTH // GMLP_GROUP_DIM
DN_WIDTH = D_MODEL // 2
DN_HEAD_DIM = 128
DN_HEADS = DN_WIDTH // DN_HEAD_DIM
CONV_WIDTH = 4
NORM_EPS = 1e-6
IN_SIZES = (GMLP_WIDTH, GMLP_WIDTH, GMLP_WIDTH, 3 * DN_WIDTH, DN_WIDTH, DN_HEADS, DN_HEADS, D_MODEL, D_MODEL)
N_IN_COLS = 3 * GMLP_WIDTH + 4 * DN_WIDTH + 2 * DN_HEADS + 2 * D_MODEL

kernel_name = "hybrid_gmlp_gated_deltanet_trunk"


def rms_norm(x, g):
    xf = x.astype(jnp.float32)
    y = xf * lax.rsqrt(jnp.mean(xf * xf, axis=-1, keepdims=True) + NORM_EPS)
    return (y * g.astype(jnp.float32)).astype(x.dtype)


def layer_norm(x, g, b):
    xf = x.astype(jnp.float32)
    mu = jnp.mean(xf, axis=-1, keepdims=True)
    xc = xf - mu
    y = xc * lax.rsqrt(jnp.mean(xc * xc, axis=-1, keepdims=True) + NORM_EPS)
    return (y * g.astype(jnp.float32) + b.astype(jnp.float32)).astype(x.dtype)


def l2_normalize(x):
    return x * lax.rsqrt(jnp.sum(x * x, axis=-1, keepdims=True) + NORM_EPS)


def split_columns(proj):
    parts, start = [], 0
    for size in IN_SIZES:
        parts.append(proj[..., start:start + size])
        start += size
    return parts


def causal_depthwise_conv(x, w):
    c = x.shape[-1]
    return lax.conv_general_dilated(
        x, w[:, None, :], window_strides=(1,), padding=[(CONV_WIDTH - 1, 0)],
        dimension_numbers=("NWC", "WIO", "NWC"), feature_group_count=c)


def gmlp_spatial_gating(u, v, ln_g, ln_b, w_s, b_s):
    bsz, seq, _ = u.shape
    nb = seq // GMLP_BLOCK
    v = layer_norm(v, ln_g, ln_b)
    chunk_id = jnp.arange(GMLP_BLOCK) // CHUNK
    mask = chunk_id[None, :] <= chunk_id[:, None]
    ws = jnp.where(mask[None], w_s, jnp.zeros((), w_s.dtype))
    vb = v.reshape(bsz, nb, GMLP_BLOCK, GMLP_GROUPS, GMLP_GROUP_DIM)
    mixed = jnp.einsum("gpq,bnqgc->bnpgc", ws, vb) + b_s.T[None, None, :, :, None]
    return u * mixed.reshape(bsz, seq, GMLP_WIDTH)


def to_chunks(t):
    bsz, seq = t.shape[0], t.shape[1]
    t = t.reshape(bsz, seq // CHUNK, CHUNK, *t.shape[2:])
    return jnp.moveaxis(t, 2, 3)


def gated_delta_rule(q, k, v, beta, g):
    bsz, seq, nh, dk = q.shape
    dv = v.shape[-1]
    qc, kc, vc = to_chunks(q), to_chunks(k), to_chunks(v)
    bc, gcum = to_chunks(beta), jnp.cumsum(to_chunks(g), axis=-1)
    idx = jnp.arange(CHUNK)
    incl = idx[:, None] >= idx[None, :]
    strict = idx[:, None] > idx[None, :]
    decay = jnp.exp(jnp.where(incl, gcum[..., :, None] - gcum[..., None, :], -jnp.inf))
    kk = jnp.einsum("bnhid,bnhjd->bnhij", kc, kc)
    lmat = jnp.where(strict, bc[..., :, None] * kk * decay, 0.0)
    gam = jnp.exp(gcum)
    rhs = jnp.concatenate([bc[..., None] * vc, (bc * gam)[..., None] * kc], axis=-1)
    sol = lax.linalg.triangular_solve(jnp.eye(CHUNK, dtype=jnp.float32) + lmat, rhs,
                                      left_side=True, lower=True, unit_diagonal=True)
    u_new, k_cum = sol[..., :dv], sol[..., dv:]
    aqk = jnp.einsum("bnhid,bnhjd->bnhij", qc, kc) * decay
    q_dec = qc * gam[..., None]
    k_dec = kc * jnp.exp(gcum[..., -1:] - gcum)[..., None]
    g_end = gam[..., -1]

    def step(state, inp):
        u_i, kcum_i, aqk_i, qd_i, kd_i, ge_i = inp
        w = u_i - jnp.einsum("bhcd,bhde->bhce", kcum_i, state)
        o = jnp.einsum("bhcd,bhde->bhce", qd_i, state) + jnp.einsum("bhij,bhje->bhie", aqk_i, w)
        state = ge_i[..., None, None] * state + jnp.einsum("bhcd,bhce->bhde", kd_i, w)
        return state, o

    xs = tuple(jnp.moveaxis(t, 1, 0) for t in (u_new, k_cum, aqk, q_dec, k_dec, g_end))
    state0 = jnp.zeros((bsz, nh, dk, dv), jnp.float32)
    _, o = lax.scan(step, state0, xs)
    o = jnp.swapaxes(jnp.moveaxis(o, 0, 1), 2, 3)
    return o.reshape(bsz, seq, nh, dv)


def setup_inputs(seed: int = 0) -> dict:
    key = jax.random.key(seed)
    ks = jax.random.split(key, 20)
    f32 = jnp.float32
    nrm = lambda k, shape, scale: jax.random.normal(k, shape, f32) * scale
    x = nrm(ks[0], (BATCH, SEQ, D_MODEL), 1.0)
    norm_g = 1.0 + nrm(ks[1], (DEPTH, D_MODEL), 0.02)
    w_in = nrm(ks[2], (DEPTH, D_MODEL, N_IN_COLS), D_MODEL ** -0.5)
    conv_w = nrm(ks[3], (DEPTH, CONV_WIDTH, 3 * DN_WIDTH), CONV_WIDTH ** -0.5)
    a_log = jnp.log(jax.random.uniform(ks[4], (DEPTH, DN_HEADS), f32, 1.0, 16.0))
    dt = jnp.exp(jax.random.uniform(ks[5], (DEPTH, DN_HEADS), f32, np.log(1e-3), np.log(1e-1)))
    dt_bias = dt + jnp.log(-jnp.expm1(-dt))
    dn_norm_g = 1.0 + nrm(ks[6], (DEPTH, DN_HEAD_DIM), 0.02)
    ln_g = 1.0 + nrm(ks[7], (DEPTH, GMLP_WIDTH), 0.02)
    ln_b = nrm(ks[8], (DEPTH, GMLP_WIDTH), 0.02)
    w_s = nrm(ks[9], (DEPTH, GMLP_GROUPS, GMLP_BLOCK, GMLP_BLOCK), GMLP_BLOCK ** -0.5)
    b_s = 1.0 + nrm(ks[10], (DEPTH, GMLP_GROUPS, GMLP_BLOCK), 0.1)
    w_br_gmlp = nrm(ks[11], (DEPTH, GMLP_WIDTH, D_MODEL), GMLP_WIDTH ** -0.5)
    w_br_dn = nrm(ks[12], (DEPTH, DN_WIDTH, D_MODEL), DN_WIDTH ** -0.5)
    w_out = nrm(ks[13], (DEPTH, D_MODEL, D_MODEL), D_MODEL ** -0.5)
    final_g = 1.0 + nrm(ks[14], (D_MODEL,), 0.02)
    return {"x": x, "norm_g": norm_g, "w_in": w_in, "conv_w": conv_w, "a_log": a_log,
            "dt_bias": dt_bias, "dn_norm_g": dn_norm_g, "ln_g": ln_g, "ln_b": ln_b,
            "w_s": w_s, "b_s": b_s, "w_br_gmlp": w_br_gmlp, "w_br_dn": w_br_dn,
            "w_out": w_out, "final_g": final_g}


def reference(x, norm_g, w_in, conv_w, a_log, dt_bias, dn_norm_g, ln_g, ln_b, w_s, b_s,
              w_br_gmlp, w_br_dn, w_out, final_g):
    bsz, seq, _ = x.shape
    for l in range(DEPTH):
        h = rms_norm(x, norm_g[l])
        proj = h @ w_in[l]
        u_pre, v_pre, z_gm, qkv_pre, z_dn, b_pre, a_pre, gate_gm, gate_dn = split_columns(proj)

        y_gm = gmlp_spatial_gating(jax.nn.gelu(u_pre, approximate=False),
                                   jax.nn.gelu(v_pre, approximate=False),
                                   ln_g[l], ln_b[l], w_s[l], b_s[l])
        y_gm = (y_gm * jax.nn.silu(z_gm)) @ w_br_gmlp[l]

        qkv = jax.nn.silu(causal_depthwise_conv(qkv_pre, conv_w[l]))
        hs = (bsz, seq, DN_HEADS, DN_HEAD_DIM)
        q = l2_normalize(qkv[..., :DN_WIDTH].reshape(hs).astype(jnp.float32)) * (DN_HEAD_DIM ** -0.5)
        k = l2_normalize(qkv[..., DN_WIDTH:2 * DN_WIDTH].reshape(hs).astype(jnp.float32))
        v = qkv[..., 2 * DN_WIDTH:].reshape(hs).astype(jnp.float32)
        beta = jax.nn.sigmoid(b_pre.astype(jnp.float32))
        g = -jnp.exp(a_log[l].astype(jnp.float32)) * jax.nn.softplus(
            a_pre.astype(jnp.float32) + dt_bias[l].astype(jnp.float32))
        o = gated_delta_rule(q, k, v, beta, g)
        o = rms_norm(o, dn_norm_g[l]).astype(x.dtype).reshape(bsz, seq, DN_WIDTH)
        y_dn = (o * jax.nn.silu(z_dn)) @ w_br_dn[l]

        merged = jax.nn.sigmoid(gate_gm) * y_gm + jax.nn.sigmoid(gate_dn) * y_dn
        x = x + merged @ w_out[l]
    return rms_norm(x, final_g)
```

```python
import contextlib
import numpy as np
import concourse.bass as bass
import concourse.mybir as mybir
from concourse.bass_utils import run_bass_kernel_spmd

_ENV = {"LA_DBG": "0"}


F32 = mybir.dt.float32
BF16 = mybir.dt.bfloat16
AF = mybir.ActivationFunctionType
ALU = mybir.AluOpType
AX = mybir.AxisListType

SEM_ROLL = 24000


class Sync:
    def __init__(self, nc, stack):
        self.nc = nc
        self.stack = stack
        self.eng = {"pe": nc.tensor, "act": nc.scalar, "dve": nc.vector, "pool": nc.gpsimd, "sp": nc.sync}
        self.sem = {}
        self.cnt = {}
        self.nsem = 0
        self.waited = {}
        self.lastw = {}
        self.readers = {}
        self.pending = {}
        self.unsig = {}
        self.n_instr = 0
        self.bank_of = {}
        self.bank_last = {}
        self.issued = {}
        self.log = []
        self.semname = {}

    def _new_sem(self, name):
        self.nsem += 1
        s = self.stack.enter_context(self.nc.semaphore(f"s{self.nsem}_{name}"))
        self.sem[name] = s
        self.semname[s.num] = name
        self.cnt[name] = 0
        return s

    def _producer(self, name, inc):
        if name not in self.sem or self.cnt[name] + inc > SEM_ROLL:
            self._new_sem(name)
        self.cnt[name] += inc
        return (self.sem[name], self.cnt[name])

    def _wait(self, e, ev):
        if ev is None:
            return
        sem, val = ev
        if sem.num in self.issued:
            val = self.issued[sem.num]
        key = (e, sem.num)
        if self.waited.get(key, 0) >= val:
            return
        self.waited[key] = val
        self.eng[e].wait_ge(sem, val)
        self.log.append((e, 'wait', f'{self.semname[sem.num]}>={val}'))
        self.n_instr += 1

    def _flush(self, e2):
        parked = self.unsig.pop(e2, [])
        if not parked:
            return
        ev = self._producer(e2, 1)
        parked[-1][2].then_inc(ev[0], 1)
        self.log.append((e2, 'flush', f'retro-signal -> {e2}={ev[1]}'))
        for (rs, ws, _i) in parked:
            self._commit(ev, rs, ws, eng=e2)

    def _conflicts(self, e, reads, writes):
        rset, wset = set(reads), set(writes)
        banks = {self.bank_of[r] for r in rset | wset if r in self.bank_of}
        for e2 in list(self.unsig.keys()):
            if e2 == e:
                continue
            for (rs, ws, _i) in self.unsig[e2]:
                hit = (wset & (set(rs) | set(ws))) or (rset & set(ws)) or (banks & {self.bank_of[r] for r in tuple(rs) + tuple(ws) if r in self.bank_of})
                if hit:
                    self._flush(e2)
                    break

    def _deps(self, e, reads, writes):
        self._conflicts(e, reads, writes)
        for reg in list(reads) + list(writes):
            b = self.bank_of.get(reg)
            if b is not None:
                for e2, ev2 in self.bank_last.get(b, {}).items():
                    if e2 != e:
                        self._wait(e, ev2)
        for r in reads:
            self._wait(e, self.lastw.get(r))
        for w in writes:
            self._wait(e, self.lastw.get(w))
            for ev in self.readers.get(w, ()):
                self._wait(e, ev)

    def _commit(self, ev, reads, writes, eng=None):
        for reg in list(reads) + list(writes):
            b = self.bank_of.get(reg)
            if b is not None and eng is not None:
                self.bank_last.setdefault(b, {})[eng] = ev
        for r in reads:
            self.readers.setdefault(r, []).append(ev)
        for w in writes:
            self.lastw[w] = ev
            self.readers[w] = []

    def op(self, e, fn, reads=(), writes=(), signal=True):
        self._deps(e, reads, writes)
        ins = fn()
        self.n_instr += 1
        self.log.append((e, 'op', f'r={list(reads)} w={list(writes)} sig={signal}'))
        if signal:
            ev = self._producer(e, 1)
            self.log[-1] = (e, 'op', self.log[-1][2] + f' -> {e}={ev[1]}')
            ins.then_inc(ev[0], 1)
            for (rs, ws, _i) in self.unsig.pop(e, []):
                self._commit(ev, rs, ws, eng=e)
            self._commit(ev, reads, writes, eng=e)
        else:
            self.unsig.setdefault(e, []).append((tuple(reads), tuple(writes), ins))
            for w in writes:
                self.lastw[w] = ("UNSIG", e)
        return ins

    def dma(self, q, stream, fn, reads=(), writes=()):
        self._deps(q, reads, writes)
        ins = fn()
        self.n_instr += 1
        ev = self._producer("dma_" + stream, 16)
        ins.then_inc(ev[0], 16)
        self.issued[ev[0].num] = ev[1]
        self._commit(ev, reads, writes)
        return ins

    def fence(self, engines=("pe", "act", "dve", "pool")):
        for e in engines:
            for name, sem in list(self.sem.items()):
                if self.cnt[name] > 0:
                    self._wait(e, (sem, self.cnt[name]))

    def drain(self, e, regions):
        for r in regions:
            self._wait(e, self.lastw.get(r))


_orig_wait = Sync._wait


def _checked_wait(self, e, ev):
    if ev is not None and ev[0] == "UNSIG" and ev[1] == e:
        return
    if ev is not None and ev[0] == "UNSIG":
        raise RuntimeError(f"dependency on unsignalled instruction of engine {ev[1]}")
    return _orig_wait(self, e, ev)


Sync._wait = _checked_wait


EPS = 1e-6
TS = 512
FENCE_TAGS = set(filter(None, _ENV.get("FENCE_TAGS", "").split(",")))
FENCE_LN = bool(FENCE_TAGS)


def _nm(t):
    return t.name if hasattr(t, "name") else t.tensor.name


def build_A(D, NT, first_core_has_halo=True):
    KC = D // 128; GW = D // 2; GG = GW // 128; DW = D // 2; H = DW // 128
    sizes = (GW, GW, GW, 3 * DW, DW, H, H, D, D)
    off = [0]
    for s in sizes: off.append(off[-1] + s)
    NCOL = off[-1]
    NS = NT // TS
    nc = bass.Bass("TRN2", target_bir_lowering=False)
    din = lambda n, s: nc.dram_tensor(n, s, F32, kind="ExternalInput").ap()
    dout = lambda n, s: nc.dram_tensor(n, s, F32, kind="ExternalOutput").ap()
    xT = din("xT", [D, NT]); xhT = din("xhT", [D, 4]); ngc = din("ngc", [128, KC])
    w_in = din("w_in", [D, NCOL]); cw = din("cw", [128, 3 * H, 4])
    lngc = din("lngc", [128, GG]); lnbc = din("lnbc", [128, GG])
    wsT = din("wsT", [128, GG, 128]); maskT = din("maskT", [128, 128]); bsb = din("bsb", [128, GG * 128])
    w_brg = din("w_brg", [GW, D]); alc = din("alc", [H, 1]); dtc = din("dtc", [H, 1])
    G1T = dout("G1T", [D, NT]); sgdT = dout("sgdT", [D, NT]); szdT = dout("szdT", [DW, NT])
    qT = dout("qT", [DW, NT]); kT = dout("kT", [DW, NT]); vT = dout("vT", [DW, NT])
    betaT = dout("betaT", [H, NT]); gT = dout("gT", [H, NT])
    DBG = _ENV.get("LA_DBG", "1") == "1"
    DUMP_LN = DBG and _ENV.get("DUMP_LN", "0") == "1"
    if DUMP_LN:
        rawD = dout("rawD", [GG, NT // 128, 128, 128]); mx2D = dout("mx2D", [GG, NT // 128, 128, 128])
        mvD = dout("mvD", [NT // 128, 128, 2]); rsD = dout("rsD", [NT // 128, 128, 1]); yD = dout("yD", [NT // 128, 128, GW])
    if DBG:
        mixD = dout("mixD", [GW, NT]); tgD = dout("tgD", [GW, NT])
    outs_regions = []

    with contextlib.ExitStack() as st:
        S = Sync(nc, st)
        sb = lambda name, shape, dt=F32: st.enter_context(nc.sbuf_tensor(name, shape, dt))
        ones = sb("ones", [128, 128]); onesb = sb("onesb", [128, 128], BF16)
        ngt = sb("ngt", [128, KC]); cwt = sb("cwt", [128, 3 * H, 4]); lng = sb("lng", [128, GG]); lnb = sb("lnb", [128, GG])
        wsm = sb("wsm", [128, GG, 128]); wsb = sb("wsb", [128, GG, 128], BF16); mk = sb("mk", [128, 128])
        CB = sb("CB", [128, GG * 128]); nea = sb("nea", [H, 1]); dtt = sb("dtt", [H, 1])
        xs = [sb(f"xs{i}", [128, TS]) for i in range(2)]
        sq = [sb(f"sq{i}", [128, TS]) for i in range(2)]
        rstd = sb("rstd", [128, TS]); xh = sb("xh", [128, KC, 4]); sqh = sb("sqh", [128, KC, 4]); rsh = sb("rsh", [128, 4])
        hT = sb("hT", [128, KC, TS], BF16); hTh = sb("hTh", [128, KC, 4], BF16)
        NWB = 3
        wb = [sb(f"wb{i}", [128, KC, 128], BF16) for i in range(NWB)]
        vtm = sb("vtm", [128, TS // 128, GW]); YBUFS = int(_ENV.get("YBUFS", "1"))
        ybfs = [sb(f"ybf{i}", [128, GW], BF16) for i in range(YBUFS)]
        bst = sb("bst", [128, (GW + 511) // 512, 6]); mv = sb("mv", [128, 2]); rs2 = sb("rs2", [128, 1])
        NEED_VSQ = (_ENV.get("PLAIN_STATS", "0") == "1") or (DBG and DUMP_LN)
        if NEED_VSQ:
            vsq = sb("vsq", [128, GW]); msq = sb("msq", [128, 1])
        if DBG and DUMP_LN:
            rawS = sb("rawS", [128, GG, 128])
        mixT = sb("mixT", [128, GG, TS]); tgb = sb("tgb", [128, GG, TS], BF16)
        t1 = [sb(f"t1_{i}", [128, TS]) for i in range(2)]; t2 = [sb(f"t2_{i}", [128, TS]) for i in range(2)]
        ot = [sb(f"ot{i}", [128, TS]) for i in range(3)]
        cv = [sb(f"cv{i}", [128, TS + 3]) for i in range(2)]; halo = sb("halo", [128, 3 * H, 3])
        ca = [sb(f"ca{i}", [128, TS]) for i in range(2)]
        sp_t = sb("sp_t", [H, TS]); sp_u = sb("sp_u", [H, TS])
        ps = [st.enter_context(nc.psum_tensor(f"ps{i}", [128, 512], F32)) for i in range(8)]
        V, A_, P_, T_ = nc.vector, nc.scalar, nc.gpsimd, nc.tensor
        for _i in range(8): S.bank_of[f"ps{_i}"] = _i

        S.op("pool", lambda: P_.memset(ones[:], 1.0), writes=["ones"])
        S.op("pool", lambda: P_.memset(onesb[:], 1.0), writes=["onesb"])
        for (t, src, nm) in [(ngt, ngc, "ngt"), (cwt, cw, "cwt"), (lng, lngc, "lng"), (lnb, lnbc, "lnb"), (wsm, wsT, "wsm"),
                             (mk, maskT, "mk"), (CB, bsb, "CB"), (nea, alc, "nea"), (dtt, dtc, "dtt")]:
            S.dma("sp", "const", lambda: nc.sync.dma_start(out=t[:], in_=src), writes=[nm])
        for g in range(GG):
            S.op("dve", lambda: V.tensor_tensor(out=wsm[:, g, :], in0=wsm[:, g, :], in1=mk[:], op=ALU.mult),
                 reads=["wsm", "mk"], writes=["wsm"])
        S.op("dve", lambda: V.tensor_copy(out=wsb[:], in_=wsm[:]), reads=["wsm"], writes=["wsb"])
        for g4 in range(0, GG, 4):
            n = min(4, GG - g4)
            S.op("pe", lambda: T_.matmul(ps[4][:, 0:n * 128], ones[:], wsm[:, g4:g4 + n, :].rearrange("p g c -> p (g c)"),
                                         start=True, stop=True), reads=["ones", "wsm"], writes=["ps4"])
            for g in range(g4, g4 + n):
                S.op("dve", lambda: V.scalar_tensor_tensor(out=CB[:, g * 128:(g + 1) * 128], in0=ps[4][:, (g - g4) * 128:(g - g4 + 1) * 128],
                                                           scalar=lnb[:, g:g + 1], in1=CB[:, g * 128:(g + 1) * 128],
                                                           op0=ALU.mult, op1=ALU.add), reads=["ps4", "lnb", "CB"], writes=["CB"])
        S.op("act", lambda: A_.activation(out=nea[:], in_=nea[:], func=AF.Exp), reads=["nea"], writes=["nea"])
        S.op("dve", lambda: V.tensor_scalar(out=nea[:], in0=nea[:], scalar1=-1.0, scalar2=None, op0=ALU.mult), reads=["nea"], writes=["nea"])

        wcount = [0]

        def load_w(wap, c0, ncols, kcn):
            i = wcount[0] % NWB; wcount[0] += 1
            step = 8
            for k0 in range(0, kcn, step):
                k1 = min(kcn, k0 + step)
                S.dma("pool", f"w{i}", lambda: P_.dma_start(
                    out=wb[i][:, k0:k1, 0:ncols],
                    in_=wap[k0 * 128:k1 * 128, c0:c0 + ncols].rearrange("(kc p) c -> p kc c", p=128)),
                    writes=[f"wb{i}"])
            return i

        def proj(bank, wi, ncols, kcn, act, ntok=TS, t0=0):
            for kc in range(kcn):
                S.op("pe", lambda: T_.matmul(ps[bank][0:ncols, 0:ntok], wb[wi][:, kc, 0:ncols], act[:, kc, t0:t0 + ntok],
                                             start=(kc == 0), stop=(kc == kcn - 1)),
                     reads=[f"wb{wi}", "act_" + _nm(act)], writes=[f"ps{bank}"], signal=(kc == kcn - 1))

        pb = [0]
        def nextbank():
            b = pb[0] % 3; pb[0] += 1; return b
        oc = [0]
        def nextot():
            i = oc[0] % 3; oc[0] += 1; return i

        def store(dst_ap, src_tile_ap, src_region, dst_region):
            S.dma("sp", f"st_{src_region}", lambda: nc.sync.dma_start(out=dst_ap, in_=src_tile_ap), reads=[src_region], writes=[dst_region])
            outs_regions.append(dst_region)

        def rmsnorm_to(dst, src_dram, ntok, xs_, sq_, rstd_):
            for kc in range(KC):
                b = kc % 2
                S.dma("sp", f"x{b}", lambda: nc.sync.dma_start(out=xs_[b][:, 0:ntok], in_=src_dram[kc * 128:(kc + 1) * 128, :]), writes=[f"xs{b}"])
                S.op("act", lambda: A_.activation(out=sq_[b][:, 0:ntok], in_=xs_[b][:, 0:ntok], func=AF.Square), reads=[f"xs{b}"], writes=[f"sq{b}"])
                S.op("pe", lambda: T_.matmul(ps[3][:, 0:ntok], ones[:], sq_[b][:, 0:ntok], start=(kc == 0), stop=(kc == KC - 1)),
                     reads=["ones", f"sq{b}"], writes=["ps3"], signal=(kc == KC - 1))
            S.op("act", lambda: A_.activation(out=rstd_[:, 0:ntok], in_=ps[3][:, 0:ntok], func=AF.Sqrt, scale=1.0 / D, bias=EPS), reads=["ps3"], writes=["rstd"])
            S.op("dve", lambda: V.reciprocal(out=rstd_[:, 0:ntok], in_=rstd_[:, 0:ntok]), reads=["rstd"], writes=["rstd"])
            for kc in range(KC):
                b = kc % 2
                S.dma("sp", f"x{b}", lambda: nc.sync.dma_start(out=xs_[b][:, 0:ntok], in_=src_dram[kc * 128:(kc + 1) * 128, :]), writes=[f"xs{b}"])
                S.op("dve", lambda: V.scalar_tensor_tensor(out=dst[:, kc, 0:ntok], in0=xs_[b][:, 0:ntok], scalar=ngt[:, kc:kc + 1], in1=rstd_[:, 0:ntok],
                                                           op0=ALU.mult, op1=ALU.mult), reads=[f"xs{b}", "ngt", "rstd"], writes=["act_" + _nm(dst)])

        rmsnorm_to(hTh, xhT, 4, xs, sq, rstd)

        for s in range(NS):
            tsl = slice(s * TS, (s + 1) * TS)
            rmsnorm_to(hT, xT[:, tsl], TS, xs, sq, rstd)

            for vb in range(GG):
                wi = load_w(w_in, off[1] + vb * 128, 128, KC)
                for tb in range(TS // 128):
                    bank = nextbank()
                    for kc in range(KC):
                        S.op("pe", lambda: T_.matmul(ps[bank][:, 0:128], hT[:, kc, tb * 128:(tb + 1) * 128], wb[wi][:, kc, :],
                                                     start=(kc == 0), stop=(kc == KC - 1)),
                             reads=[f"wb{wi}", "act_hT"], writes=[f"ps{bank}"], signal=(kc == KC - 1))
                    S.op("act", lambda: A_.activation(out=vtm[:, tb, vb * 128:(vb + 1) * 128], in_=ps[bank][:, 0:128], func=AF.Gelu),
                         reads=[f"ps{bank}"], writes=[("vtm", tb)])
            _op = S.op
            S.op = lambda e, fn, tag=None, **kw: _op(e, fn, **kw)
            if FENCE_LN:
                def _fenced(e, fn, tag=None, **kw):
                    if tag in FENCE_TAGS or "all" in FENCE_TAGS: S.fence()
                    return _op(e, fn, **kw)
                S.op = _fenced
            for tb in range(TS // 128):
                ybf = ybfs[tb % YBUFS]; YR = f"ybf{tb % YBUFS}"
                nchunk = (GW + 511) // 512
                if _ENV.get("PLAIN_STATS", "0") == "1":
                    S.op("dve", lambda: V.reduce_sum(out=mv[:, 0:1], in_=vtm[:, tb, :], axis=AX.X), reads=[("vtm", tb)], writes=["mv"])
                    S.op("dve", lambda: V.tensor_tensor(out=vsq[:], in0=vtm[:, tb, :], in1=vtm[:, tb, :], op=ALU.mult), reads=[("vtm", tb)], writes=["vsq"])
                    S.op("dve", lambda: V.reduce_sum(out=mv[:, 1:2], in_=vsq[:], axis=AX.X), reads=["vsq", "mv"], writes=["mv"])
                    S.op("dve", lambda: V.tensor_scalar(out=mv[:], in0=mv[:], scalar1=1.0 / GW, scalar2=None, op0=ALU.mult), reads=["mv"], writes=["mv"])
                    S.op("dve", lambda: V.tensor_tensor(out=msq[:], in0=mv[:, 0:1], in1=mv[:, 0:1], op=ALU.mult), reads=["mv"], writes=["msq"])
                    S.op("dve", lambda: V.tensor_tensor(out=mv[:, 1:2], in0=mv[:, 1:2], in1=msq[:], op=ALU.subtract), reads=["mv", "msq"], writes=["mv"])
                else:
                    for c in range(nchunk):
                        S.op("dve", tag="stats", fn=lambda: V.bn_stats(out=bst[:, c, :], in_=vtm[:, tb, c * 512:min(GW, (c + 1) * 512)]), reads=[("vtm", tb)], writes=["bst"])
                    S.op("dve", tag="aggr", fn=lambda: V.bn_aggr(out=mv[:], in_=bst[:].rearrange("p c s -> p (c s)")), reads=["bst"], writes=["mv"])
                S.op("act", tag="sqrt", fn=lambda: A_.activation(out=rs2[:], in_=mv[:, 1:2], func=AF.Sqrt, bias=EPS), reads=["mv"], writes=["rs2"])
                S.op("dve", tag="recip", fn=lambda: V.reciprocal(out=rs2[:], in_=rs2[:]), reads=["rs2"], writes=["rs2"])
                S.op("dve", tag="norm", fn=lambda: V.tensor_scalar(out=ybf[:], in0=vtm[:, tb, :], scalar1=mv[:, 0:1], scalar2=rs2[:, 0:1], op0=ALU.subtract, op1=ALU.mult),
                     reads=[("vtm", tb), "mv", "rs2"], writes=[YR])
                if DUMP_LN:
                    gtb = s * (TS // 128) + tb
                    store(mvD[gtb], mv[:], "mv", ("mvD", gtb))
                    store(rsD[gtb], rs2[:], "rs2", ("rsD", gtb))
                    S.op("dve", lambda: V.tensor_copy(out=vsq[:], in_=ybf[:]), reads=[YR], writes=["vsq"])
                    store(yD[gtb], vsq[:], "vsq", ("yD", gtb))
                for g4 in range(0, GG, 4):
                    n = min(4, GG - g4); bank = 4 + (g4 // 4) % 2
                    PE_DRAIN = _ENV.get("PE_DRAIN", "0") == "1"
                    for g in range(g4, g4 + n):
                        S.op("pe", tag="mm", fn=lambda: T_.matmul(ps[bank][:, (g - g4) * 128:(g - g4 + 1) * 128], ybf[:, g * 128:(g + 1) * 128], wsb[:, g, :], start=True, stop=True),
                             reads=[YR, "wsb"], writes=[f"ps{bank}"], signal=(g == g4 + n - 1) and not PE_DRAIN)
                    if PE_DRAIN:
                        S.op("pe", tag="mm", fn=lambda: T_.matmul(ps[6][:, 16:32], onesb[:, 0:128], onesb[:, 0:16], start=True, stop=True),
                             reads=["onesb"], writes=["ps6"], signal=True)
                    for g in range(g4, g4 + n):
                        if DUMP_LN:
                            gtb2 = s * (TS // 128) + tb
                            S.op("dve", lambda: V.tensor_copy(out=rawS[:, g, :], in_=ps[bank][:, (g - g4) * 128:(g - g4 + 1) * 128]), reads=[f"ps{bank}"], writes=[("rawS", g)])
                            store(rawD[g, gtb2], rawS[:, g, :], ("rawS", g), ("rawD", g, gtb2))
                        S.op("dve", tag="evac", fn=lambda: V.scalar_tensor_tensor(out=mixT[:, g, tb * 128:(tb + 1) * 128], in0=ps[bank][:, (g - g4) * 128:(g - g4 + 1) * 128],
                                                                   scalar=lng[:, g:g + 1], in1=CB[:, g * 128:(g + 1) * 128], op0=ALU.mult, op1=ALU.add),
                             reads=[f"ps{bank}", "lng", "CB"], writes=[("mixT", g)])
                        if DUMP_LN:
                            store(mx2D[g, gtb2], mixT[:, g, tb * 128:(tb + 1) * 128], ("mixT", g), ("mx2D", g, gtb2))

            S.op = _op
            if "all" in FENCE_TAGS: S.fence()
            for g in range(GG):
                wu = load_w(w_in, off[0] + g * 128, 128, KC); bu = nextbank(); proj(bu, wu, 128, KC, hT)
                wz = load_w(w_in, off[2] + g * 128, 128, KC); bz = nextbank(); proj(bz, wz, 128, KC, hT)
                i = g % 2
                S.op("act", lambda: A_.activation(out=t1[i][:], in_=ps[bu][:], func=AF.Gelu), reads=[f"ps{bu}"], writes=[f"t1_{i}"])
                S.op("act", lambda: A_.activation(out=t2[i][:], in_=ps[bz][:], func=AF.Silu), reads=[f"ps{bz}"], writes=[f"t2_{i}"])
                S.op("dve", lambda: V.tensor_tensor(out=t1[i][:], in0=t1[i][:], in1=mixT[:, g, :], op=ALU.mult), reads=[f"t1_{i}", ("mixT", g)], writes=[f"t1_{i}"])
                S.op("dve", lambda: V.tensor_tensor(out=tgb[:, g, :], in0=t1[i][:], in1=t2[i][:], op=ALU.mult), reads=[f"t1_{i}", f"t2_{i}"], writes=["act_tgb"])

            if DBG:
                tgf = ot
                for g in range(GG):
                    store(mixD[g * 128:(g + 1) * 128, tsl], mixT[:, g, :], ("mixT", g), ("mixD", g, s))
                    o = nextot()
                    S.op("dve", lambda: V.tensor_copy(out=ot[o][:], in_=tgb[:, g, :]), reads=["act_tgb"], writes=[f"ot{o}"])
                    store(tgD[g * 128:(g + 1) * 128, tsl], ot[o][:], f"ot{o}", ("tgD", g, s))
            for ob in range(KC):
                wy = load_w(w_brg, ob * 128, 128, GG)
                for kc in range(GG):
                    S.op("pe", lambda: T_.matmul(ps[7][:], wb[wy][:, kc, :], tgb[:, kc, :], start=(kc == 0), stop=(kc == GG - 1)),
                         reads=[f"wb{wy}", "act_tgb"], writes=["ps7"], signal=(kc == GG - 1))
                wg = load_w(w_in, off[7] + ob * 128, 128, KC); bg = nextbank(); proj(bg, wg, 128, KC, hT)
                i = ob % 2; o = nextot()
                S.op("act", lambda: A_.activation(out=t2[i][:], in_=ps[bg][:], func=AF.Sigmoid), reads=[f"ps{bg}"], writes=[f"t2_{i}"])
                S.op("dve", lambda: V.tensor_tensor(out=ot[o][:], in0=ps[7][:], in1=t2[i][:], op=ALU.mult), reads=["ps7", f"t2_{i}"], writes=[f"ot{o}"])
                store(G1T[ob * 128:(ob + 1) * 128, tsl], ot[o][:], f"ot{o}", ("G1T", ob, s))

            for cb in range(3 * H):
                wq = load_w(w_in, off[3] + cb * 128, 128, KC); bq = nextbank(); proj(bq, wq, 128, KC, hT)
                i = cb % 2
                if s == 0:
                    for kc in range(KC):
                        S.op("pe", lambda: T_.matmul(ps[6][:, 0:4], wb[wq][:, kc, :], hTh[:, kc, :], start=(kc == 0), stop=(kc == KC - 1)),
                             reads=[f"wb{wq}", "act_hTh"], writes=["ps6"], signal=(kc == KC - 1))
                    S.op("dve", lambda: V.tensor_copy(out=cv[i][:, 0:3], in_=ps[6][:, 0:3]), reads=["ps6"], writes=[f"cv{i}"])
                else:
                    S.op("dve", lambda: V.tensor_copy(out=cv[i][:, 0:3], in_=halo[:, cb, :]), reads=[("halo", cb)], writes=[f"cv{i}"])
                S.op("act", lambda: A_.copy(out=cv[i][:, 3:TS + 3], in_=ps[bq][:]), reads=[f"ps{bq}"], writes=[f"cv{i}"])
                S.op("dve", lambda: V.tensor_copy(out=halo[:, cb, :], in_=cv[i][:, TS:TS + 3]), reads=[f"cv{i}"], writes=[("halo", cb)])
                S.op("dve", lambda: V.tensor_scalar(out=ca[i][:], in0=cv[i][:, 0:TS], scalar1=cwt[:, cb, 0:1], scalar2=None, op0=ALU.mult),
                     reads=[f"cv{i}", "cwt"], writes=[f"ca{i}"])
                for j in range(1, 4):
                    S.op("dve", lambda: V.scalar_tensor_tensor(out=ca[i][:], in0=cv[i][:, j:j + TS], scalar=cwt[:, cb, j:j + 1], in1=ca[i][:], op0=ALU.mult, op1=ALU.add),
                         reads=[f"cv{i}", "cwt", f"ca{i}"], writes=[f"ca{i}"])
                o = nextot()
                kind = cb // H
                if kind == 2:
                    S.op("act", lambda: A_.activation(out=ot[o][:], in_=ca[i][:], func=AF.Silu), reads=[f"ca{i}"], writes=[f"ot{o}"])
                    store(vT[(cb - 2 * H) * 128:(cb - 2 * H + 1) * 128, tsl], ot[o][:], f"ot{o}", ("vT", cb, s))
                else:
                    S.op("act", lambda: A_.activation(out=ca[i][:], in_=ca[i][:], func=AF.Silu), reads=[f"ca{i}"], writes=[f"ca{i}"])
                    S.op("act", lambda: A_.activation(out=t1[i][:], in_=ca[i][:], func=AF.Square), reads=[f"ca{i}"], writes=[f"t1_{i}"])
                    S.op("pe", lambda: T_.matmul(ps[3][:], ones[:], t1[i][:], start=True, stop=True), reads=["ones", f"t1_{i}"], writes=["ps3"])
                    S.op("act", lambda: A_.activation(out=t1[i][:], in_=ps[3][:], func=AF.Sqrt, bias=EPS), reads=["ps3"], writes=[f"t1_{i}"])
                    S.op("dve", lambda: V.reciprocal(out=t1[i][:], in_=t1[i][:]), reads=[f"t1_{i}"], writes=[f"t1_{i}"])
                    sc = (128 ** -0.5) if kind == 0 else 1.0
                    S.op("dve", lambda: V.scalar_tensor_tensor(out=ot[o][:], in0=ca[i][:], scalar=sc, in1=t1[i][:], op0=ALU.mult, op1=ALU.mult),
                         reads=[f"ca{i}", f"t1_{i}"], writes=[f"ot{o}"])
                    dst = qT if kind == 0 else kT
                    hb = cb - kind * H
                    store(dst[hb * 128:(hb + 1) * 128, tsl], ot[o][:], f"ot{o}", ("qk", cb, s))

            for zb in range(DW // 128):
                wz = load_w(w_in, off[4] + zb * 128, 128, KC); bz = nextbank(); proj(bz, wz, 128, KC, hT); o = nextot()
                S.op("act", lambda: A_.activation(out=ot[o][:], in_=ps[bz][:], func=AF.Silu), reads=[f"ps{bz}"], writes=[f"ot{o}"])
                store(szdT[zb * 128:(zb + 1) * 128, tsl], ot[o][:], f"ot{o}", ("szd", zb, s))
            for gb in range(KC):
                wg = load_w(w_in, off[8] + gb * 128, 128, KC); bg = nextbank(); proj(bg, wg, 128, KC, hT); o = nextot()
                S.op("act", lambda: A_.activation(out=ot[o][:], in_=ps[bg][:], func=AF.Sigmoid), reads=[f"ps{bg}"], writes=[f"ot{o}"])
                store(sgdT[gb * 128:(gb + 1) * 128, tsl], ot[o][:], f"ot{o}", ("sgd", gb, s))

            wbt = load_w(w_in, off[5], H, KC); bb = nextbank(); proj(bb, wbt, H, KC, hT); o = nextot()
            S.op("act", lambda: A_.activation(out=ot[o][0:H, :], in_=ps[bb][0:H, :], func=AF.Sigmoid), reads=[f"ps{bb}"], writes=[f"ot{o}"])
            store(betaT[:, tsl], ot[o][0:H, :], f"ot{o}", ("beta", s))
            wa = load_w(w_in, off[6], H, KC); ba = nextbank(); proj(ba, wa, H, KC, hT); o = nextot()
            S.op("dve", lambda: V.tensor_scalar(out=sp_t[:], in0=ps[ba][0:H, :], scalar1=dtt[:, 0:1], scalar2=None, op0=ALU.add), reads=[f"ps{ba}", "dtt"], writes=["sp_t"])
            S.op("act", lambda: A_.activation(out=sp_u[:], in_=sp_t[:], func=AF.Abs), reads=["sp_t"], writes=["sp_u"])
            S.op("act", lambda: A_.activation(out=sp_u[:], in_=sp_u[:], func=AF.Exp, scale=-1.0), reads=["sp_u"], writes=["sp_u"])
            S.op("act", lambda: A_.activation(out=sp_u[:], in_=sp_u[:], func=AF.Ln, bias=1.0), reads=["sp_u"], writes=["sp_u"])
            S.op("dve", lambda: V.scalar_tensor_tensor(out=sp_t[:], in0=sp_t[:], scalar=0.0, in1=sp_u[:], op0=ALU.max, op1=ALU.add), reads=["sp_t", "sp_u"], writes=["sp_t"])
            S.op("dve", lambda: V.tensor_scalar(out=ot[o][0:H, :], in0=sp_t[:], scalar1=nea[:, 0:1], scalar2=None, op0=ALU.mult), reads=["sp_t", "nea"], writes=[f"ot{o}"])
            store(gT[:, tsl], ot[o][0:H, :], f"ot{o}", ("g", s))

        S.drain("sp", outs_regions)
        print("launch A: instrs", S.n_instr, "sems", S.nsem)
    return nc


def host_inputs_A(cfg_D, p, l, x_full, c, NT):
    D = cfg_D; KC = D // 128; GW = D // 2; GG = GW // 128; H = GW // 128
    f = lambda a: np.ascontiguousarray(a, dtype=np.float32)
    t0 = c * NT
    xh = np.zeros((4, D), np.float32)
    if c > 0: xh[0:3] = x_full[t0 - 3:t0]
    cid = np.arange(128) // 64
    return {
        "xT": f(x_full[t0:t0 + NT].T), "xhT": f(xh.T), "ngc": f(p["norm_g"][l].reshape(KC, 128).T),
        "w_in": f(p["w_in"][l]), "cw": f(p["conv_w"][l].T.reshape(3 * H, 128, 4).transpose(1, 0, 2)),
        "lngc": f(p["ln_g"][l].reshape(GG, 128).T), "lnbc": f(p["ln_b"][l].reshape(GG, 128).T),
        "wsT": f(p["w_s"][l].transpose(2, 0, 1)), "maskT": f((cid[:, None] <= cid[None, :])),
        "bsb": f(np.broadcast_to(p["b_s"][l].reshape(1, GG * 128), (128, GG * 128))),
        "w_brg": f(p["w_br_gmlp"][l]), "alc": f(p["a_log"][l].reshape(H, 1)), "dtc": f(p["dt_bias"][l].reshape(H, 1)),
    }


C = 128
STOP = int(_ENV.get('STOP_B', '99'))
SL = 8


def build_B(S_len, HB):
    NCH = S_len // C
    nc = bass.Bass("TRN2", target_bir_lowering=False)
    din = lambda n, s: nc.dram_tensor(n, s, F32, kind="ExternalInput").ap()
    qT = din("qT", [HB, 128, S_len]); kT = din("kT", [HB, 128, S_len]); vT = din("vT", [HB, 128, S_len])
    bM = din("bM", [HB, 128, NCH]); gM = din("gM", [HB, 128, NCH])
    ktm = din("ktm", [HB, S_len, 128]); vtm = din("vtm", [HB, S_len, 128])
    cst = din("cst", [7, 128, 128])
    o = nc.dram_tensor("o", [HB, S_len, 128], F32, kind="ExternalOutput").ap()
    outs = []
    with contextlib.ExitStack() as st:
        Sy = Sync(nc, st)
        sb = lambda name, shape, dt=F32: st.enter_context(nc.sbuf_tensor(name, shape, dt))
        V, A_, T_ = nc.vector, nc.scalar, nc.tensor
        ident = sb("ident", [128, 128]); ones = sb("ones", [128, 128]); UT = sb("UT", [128, 128])
        mLs = sb("mLs", [128, 128]); mLi = sb("mLi", [128, 128]); El = sb("El", [128, 128]); mUs = sb("mUs", [128, 128])
        for i, (t, nm) in enumerate([(ident, "ident"), (ones, "ones"), (UT, "UT"), (mLs, "mLs"), (mLi, "mLi"), (El, "El"), (mUs, "mUs")]):
            Sy.dma("sp", f"c{i}", lambda: nc.sync.dma_start(out=t[:], in_=cst[i]), writes=[nm])
        banks = [st.enter_context(nc.psum_tensor(f"pb{i}", [128, 512], F32)) for i in range(8)]
        slot_i = [0]
        def pslot():
            i = slot_i[0] % 32; slot_i[0] += 1
            b, q = i % 8, i // 8
            Sy.bank_of[f"pslot{i}"] = b
            return banks[b][:, q * 128:(q + 1) * 128], f"pslot{i}"

        per = {}
        for h in range(HB):
            P = {}
            for nm in ("beta", "gcum", "gam", "kdsc", "geB", "bgam", "nbeta", "glB", "graw"):
                P[nm] = sb(f"{nm}{h}", [128, NCH])
            P["S"] = [sb(f"S{h}_{i}", [128, 128]) for i in range(2)]
            P["q"] = [sb(f"q{h}_{i}", [128, SL * C]) for i in range(2)]
            P["k"] = [sb(f"k{h}_{i}", [128, SL * C]) for i in range(2)]
            P["v"] = [sb(f"v{h}_{i}", [128, SL * C]) for i in range(2)]
            P["kt"] = [sb(f"kt{h}_{i}", [128, SL, 128]) for i in range(2)]
            P["vt"] = [sb(f"vt{h}_{i}", [128, SL, 128]) for i in range(2)]
            for nm in ("bgk", "kd", "bv", "diagG", "tA", "decS", "tB", "decB", "AqkT", "kcT", "usb", "w", "awsb", "osb", "diagNB", "decBs", "ytmp"):
                P[nm] = sb(f"{nm}{h}", [128, 128])
            P["X"] = [sb(f"X{h}_{i}", [128, 128]) for i in range(2)]
            P["Y"] = [sb(f"Y{h}_{i}", [128, 128]) for i in range(2)]
            P["P"] = [sb(f"P{h}_{i}", [128, 128]) for i in range(2)]
            per[h] = P
            R = lambda nm: f"{nm}{h}"
            Sy.dma("sp", f"bg{h}", lambda: nc.sync.dma_start(out=P["beta"][:], in_=bM[h]), writes=[R("beta")])
            Sy.dma("sp", f"bg{h}", lambda: nc.sync.dma_start(out=P["graw"][:], in_=gM[h]), writes=[R("graw")])
            ps, pr = pslot()
            Sy.op("pe", lambda: T_.matmul(ps[:, 0:NCH], UT[:], P["graw"][:], start=True, stop=True), reads=["UT", R("graw")], writes=[pr])
            Sy.op("act", lambda: A_.copy(out=P["gcum"][:], in_=ps[:, 0:NCH]), reads=[pr], writes=[R("gcum")])
            ps2, pr2 = pslot()
            Sy.op("pe", lambda: T_.matmul(ps2[:, 0:NCH], El[:], P["gcum"][:], start=True, stop=True), reads=["El", R("gcum")], writes=[pr2])
            Sy.op("act", lambda: A_.copy(out=P["glB"][:], in_=ps2[:, 0:NCH]), reads=[pr2], writes=[R("glB")])
            Sy.op("act", lambda: A_.activation(out=P["gam"][:], in_=P["gcum"][:], func=AF.Exp), reads=[R("gcum")], writes=[R("gam")])
            Sy.op("act", lambda: A_.activation(out=P["geB"][:], in_=P["glB"][:], func=AF.Exp), reads=[R("glB")], writes=[R("geB")])
            Sy.op("dve", lambda: V.tensor_tensor(out=P["kdsc"][:], in0=P["glB"][:], in1=P["gcum"][:], op=ALU.subtract), reads=[R("glB"), R("gcum")], writes=[R("kdsc")])
            Sy.op("act", lambda: A_.activation(out=P["kdsc"][:], in_=P["kdsc"][:], func=AF.Exp), reads=[R("kdsc")], writes=[R("kdsc")])
            Sy.op("dve", lambda: V.tensor_tensor(out=P["bgam"][:], in0=P["beta"][:], in1=P["gam"][:], op=ALU.mult), reads=[R("beta"), R("gam")], writes=[R("bgam")])
            Sy.op("dve", lambda: V.tensor_scalar(out=P["nbeta"][:], in0=P["beta"][:], scalar1=-1.0, scalar2=None, op0=ALU.mult), reads=[R("beta")], writes=[R("nbeta")])
            Sy.op("dve", lambda: V.memset(P["S"][0][:], 0.0), writes=[R("S0")])

        for n in range(NCH if STOP > 0 else 0):
            for h in range(HB):
                P = per[h]; R = lambda nm: f"{nm}{h}"
                sl = (n // SL) % 2; c0 = (n % SL) * C
                if n % SL == 0:
                    t0 = n * C; t1 = min(S_len, t0 + SL * C)
                    for nm, src in (("q", qT), ("k", kT), ("v", vT)):
                        Sy.dma("sp", f"{nm}{h}_{sl}", lambda: nc.sync.dma_start(out=P[nm][sl][:, 0:t1 - t0], in_=src[h][:, t0:t1]), writes=[R(f"{nm}sl{sl}")])
                    nsl = (t1 - t0) // C
                    for nm, src in (("kt", ktm), ("vt", vtm)):
                        Sy.dma("sp", f"{nm}{h}_{sl}", lambda: nc.sync.dma_start(out=P[nm][sl][:, 0:nsl, :], in_=src[h][t0:t1, :].rearrange("(s c) d -> c s d", c=C)), writes=[R(f"{nm}sl{sl}")])
                qc = P["q"][sl][:, c0:c0 + C]; kc = P["k"][sl][:, c0:c0 + C]; vc = P["v"][sl][:, c0:c0 + C]
                Rq, Rk, Rv = R(f"qsl{sl}"), R(f"ksl{sl}"), R(f"vsl{sl}")
                col = lambda nm: P[nm][:, n:n + 1]
                ktc = P["kt"][sl][:, n % SL, :]; vtc = P["vt"][sl][:, n % SL, :]; Rkt, Rvt = R(f"ktsl{sl}"), R(f"vtsl{sl}")
                Sy.op("dve", lambda: V.tensor_scalar(out=P["bgk"][:], in0=ktc, scalar1=col("bgam"), scalar2=None, op0=ALU.mult), reads=[Rkt, R("bgam")], writes=[R("bgk")])
                Sy.op("dve", lambda: V.tensor_scalar(out=P["kd"][:], in0=ktc, scalar1=col("kdsc"), scalar2=None, op0=ALU.mult), reads=[Rkt, R("kdsc")], writes=[R("kd")])
                Sy.op("dve", lambda: V.tensor_scalar(out=P["bv"][:], in0=vtc, scalar1=col("beta"), scalar2=None, op0=ALU.mult), reads=[Rvt, R("beta")], writes=[R("bv")])
                if STOP <= 1: continue
                Sy.op("dve", lambda: V.tensor_scalar(out=P["diagG"][:], in0=ident[:], scalar1=col("gcum"), scalar2=None, op0=ALU.mult), reads=["ident", R("gcum")], writes=[R("diagG")])
                Sy.op("dve", lambda: V.tensor_scalar(out=P["diagNB"][:], in0=ident[:], scalar1=col("nbeta"), scalar2=None, op0=ALU.mult), reads=["ident", R("nbeta")], writes=[R("diagNB")])
                pR, rR = pslot(); pKK, rKK = pslot(); pQK, rQK = pslot(); pRB, rRB = pslot()
                Sy.op("pe", lambda: T_.matmul(pRB, ones[:], P["diagNB"][:], start=True, stop=True), reads=["ones", R("diagNB")], writes=[rRB])
                Sy.op("pe", lambda: T_.matmul(pR, ones[:], P["diagG"][:], start=True, stop=True), reads=["ones", R("diagG")], writes=[rR])
                Sy.op("pe", lambda: T_.matmul(pKK, kc, kc, start=True, stop=True), reads=[Rk], writes=[rKK])
                Sy.op("pe", lambda: T_.matmul(pQK, kc, qc, start=True, stop=True), reads=[Rk, Rq], writes=[rQK])
                if STOP <= 2: continue
                Sy.op("dve", lambda: V.tensor_scalar(out=P["ytmp"][:], in0=pR, scalar1=col("gcum"), scalar2=None, op0=ALU.subtract), reads=[rR, R("gcum")], writes=[R("ytmp")])
                Sy.op("dve", lambda: V.tensor_scalar_max(out=P["tA"][:], in0=P["ytmp"][:], scalar1=0.0), reads=[R("ytmp")], writes=[R("tA")])
                Sy.op("dve", lambda: V.tensor_scalar_min(out=P["tB"][:], in0=P["ytmp"][:], scalar1=0.0), reads=[R("ytmp")], writes=[R("tB")])
                Sy.op("act", lambda: A_.activation(out=P["tA"][:], in_=P["tA"][:], func=AF.Exp, scale=-1.0), reads=[R("tA")], writes=[R("tA")])
                Sy.op("dve", lambda: V.tensor_tensor(out=P["decS"][:], in0=P["tA"][:], in1=mLs[:], op=ALU.mult), reads=[R("tA"), "mLs"], writes=[R("decS")])
                Sy.op("act", lambda: A_.activation(out=P["tB"][:], in_=P["tB"][:], func=AF.Exp), reads=[R("tB")], writes=[R("tB")])
                Sy.op("dve", lambda: V.tensor_tensor(out=P["decB"][:], in0=P["tB"][:], in1=UT[:], op=ALU.mult), reads=[R("tB"), "UT"], writes=[R("decB")])
                Sy.op("dve", lambda: V.tensor_tensor(out=P["decBs"][:], in0=P["tB"][:], in1=mUs[:], op=ALU.mult), reads=[R("tB"), "mUs"], writes=[R("decBs")])
                Sy.op("dve", lambda: V.scalar_tensor_tensor(out=P["X"][0][:], in0=pKK, scalar=col("nbeta"), in1=P["decS"][:], op0=ALU.mult, op1=ALU.mult),
                      reads=[rKK, R("nbeta"), R("decS")], writes=[R("X0")])
                Sy.op("dve", lambda: V.tensor_tensor(out=P["AqkT"][:], in0=pQK, in1=P["decB"][:], op=ALU.mult), reads=[rQK, R("decB")], writes=[R("AqkT")])
                if STOP <= 3: continue
                Sy.op("dve", lambda: V.tensor_tensor(out=P["ytmp"][:], in0=pKK, in1=P["decBs"][:], op=ALU.mult), reads=[rKK, R("decBs")], writes=[R("ytmp")])
                Sy.op("dve", lambda: V.tensor_tensor(out=P["Y"][0][:], in0=pRB, in1=P["ytmp"][:], op=ALU.mult), reads=[rRB, R("ytmp")], writes=[R("Y0")])
                Sy.op("dve", lambda: V.tensor_tensor(out=P["P"][0][:], in0=P["Y"][0][:], in1=ident[:], op=ALU.add), reads=[R("Y0"), "ident"], writes=[R("P0")])
                NLEV = 6
                for m in range(NLEV):
                    a, b = m % 2, (m + 1) % 2
                    pX, rX = pslot()
                    Sy.op("pe", lambda: T_.matmul(pX, P["Y"][a][:], P["X"][a][:], start=True, stop=True), reads=[R(f"Y{a}"), R(f"X{a}")], writes=[rX])
                    if m < NLEV - 1:
                        pY2, rY2 = pslot()
                        Sy.op("pe", lambda: T_.matmul(pY2, P["X"][a][:], P["Y"][a][:], start=True, stop=True), reads=[R(f"X{a}"), R(f"Y{a}")], writes=[rY2])
                    Sy.op("act", lambda: A_.copy(out=P["X"][b][:], in_=pX), reads=[rX], writes=[R(f"X{b}")])
                    if m < NLEV - 1:
                        Sy.op("dve", lambda: V.tensor_copy(out=P["Y"][b][:], in_=pY2), reads=[rY2], writes=[R(f"Y{b}")])
                    pP, rP = pslot()
                    Sy.op("pe", lambda: T_.matmul(pP, P["X"][b][:], P["P"][a][:], start=True, stop=True), reads=[R(f"X{b}"), R(f"P{a}")], writes=[rP])
                    Sy.op("dve", lambda: V.tensor_tensor(out=P["P"][b][:], in0=pP, in1=P["P"][a][:], op=ALU.add), reads=[rP, R(f"P{a}")], writes=[R(f"P{b}")])
                Mi = P["P"][NLEV % 2]; RMi = R(f"P{NLEV % 2}")
                if STOP <= 4: continue
                pKc, rKc = pslot(); pU, rU = pslot()
                Sy.op("pe", lambda: T_.matmul(pKc, P["bgk"][:], Mi[:], start=True, stop=True), reads=[R("bgk"), RMi], writes=[rKc])
                Sy.op("pe", lambda: T_.matmul(pU, Mi[:], P["bv"][:], start=True, stop=True), reads=[RMi, R("bv")], writes=[rU])
                Sy.op("act", lambda: A_.copy(out=P["kcT"][:], in_=pKc), reads=[rKc], writes=[R("kcT")])
                Sy.op("act", lambda: A_.copy(out=P["usb"][:], in_=pU), reads=[rU], writes=[R("usb")])
                if STOP <= 5: continue
                sa, sbn = n % 2, (n + 1) % 2
                Scur = P["S"][sa]; RS = R(f"S{sa}")
                pW, rW = pslot(); pQS, rQS = pslot()
                Sy.op("pe", lambda: T_.matmul(pW, P["kcT"][:], Scur[:], start=True, stop=True), reads=[R("kcT"), RS], writes=[rW])
                Sy.op("pe", lambda: T_.matmul(pQS, qc, Scur[:], start=True, stop=True), reads=[Rq, RS], writes=[rQS])
                Sy.op("dve", lambda: V.scalar_tensor_tensor(out=P["w"][:], in0=pW, scalar=-1.0, in1=P["usb"][:], op0=ALU.mult, op1=ALU.add), reads=[R("usb"), rW], writes=[R("w")])
                pAW, rAW = pslot(); pS, rS = pslot()
                Sy.op("pe", lambda: T_.matmul(pAW, P["AqkT"][:], P["w"][:], start=True, stop=True), reads=[R("AqkT"), R("w")], writes=[rAW])
                Sy.op("pe", lambda: T_.matmul(pS, P["kd"][:], P["w"][:], start=True, stop=True), reads=[R("kd"), R("w")], writes=[rS])
                Sy.op("act", lambda: A_.copy(out=P["awsb"][:], in_=pAW), reads=[rAW], writes=[R("awsb")])
                Sy.op("dve", lambda: V.scalar_tensor_tensor(out=P["osb"][:], in0=pQS, scalar=col("gam"), in1=P["awsb"][:], op0=ALU.mult, op1=ALU.add),
                      reads=[rQS, R("gam"), R("awsb")], writes=[R("osb")])
                Sy.dma("sp", f"o{h}", lambda: nc.sync.dma_start(out=o[h][n * C:(n + 1) * C, :], in_=P["osb"][:]), reads=[R("osb")], writes=[("o", h, n)])
                outs.append(("o", h, n))
                Sy.op("dve", lambda: V.tensor_scalar(out=P["ytmp"][:], in0=Scur[:], scalar1=col("geB"), scalar2=None, op0=ALU.mult), reads=[RS, R("geB")], writes=[R("ytmp")])
                Sy.op("dve", lambda: V.tensor_tensor(out=P["S"][sbn][:], in0=pS, in1=P["ytmp"][:], op=ALU.add), reads=[rS, R("ytmp")], writes=[R(f"S{sbn}")])
        Sy.drain("sp", outs)
        print("launch B: instrs", Sy.n_instr, "sems", Sy.nsem)
    return nc


def consts_B():
    i = np.arange(128)
    ident = np.eye(128); ones = np.ones((128, 128)); UT = (i[:, None] <= i[None, :])
    mLs = (i[None, :] < i[:, None]); mLi = (i[None, :] <= i[:, None]); El = np.zeros((128, 128)); El[127, :] = 1
    mUs = (i[:, None] < i[None, :])
    return np.ascontiguousarray(np.stack([ident, ones, UT, mLs, mLi, El, mUs]).astype(np.float32))


def host_inputs_B(q, k, v, beta, g, heads):
    f = lambda a: np.ascontiguousarray(a, dtype=np.float32)
    S_len = q.shape[0]; NCH = S_len // C
    return {"qT": f(np.stack([q[:, h].T for h in heads])), "kT": f(np.stack([k[:, h].T for h in heads])),
            "vT": f(np.stack([v[:, h].T for h in heads])),
            "ktm": f(np.stack([k[:, h] for h in heads])), "vtm": f(np.stack([v[:, h] for h in heads])),
            "bM": f(np.stack([beta[:, h].reshape(NCH, C).T for h in heads])),
            "gM": f(np.stack([g[:, h].reshape(NCH, C).T for h in heads])), "cst": consts_B()}


EPS = 1e-6
TS = 512


def build_C(D, NT, last):
    KC = D // 128; DW = D // 2; HK = DW // 128
    NS = NT // TS
    nc = bass.Bass("TRN2", target_bir_lowering=False)
    din = lambda n, s: nc.dram_tensor(n, s, F32, kind="ExternalInput").ap()
    xT = din("xT", [D, NT]); oT = din("oT", [DW, NT]); szdT = din("szdT", [DW, NT]); sgdT = din("sgdT", [D, NT]); G1T = din("G1T", [D, NT])
    w_brd = din("w_brd", [DW, D]); w_out = din("w_out", [D, D]); dngc = din("dngc", [128, 1]); fgc = din("fgc", [128, KC])
    yT = nc.dram_tensor("yT", [D, NT], F32, kind="ExternalOutput").ap()
    outs = []
    with contextlib.ExitStack() as st:
        S = Sync(nc, st)
        sb = lambda name, shape, dt=F32: st.enter_context(nc.sbuf_tensor(name, shape, dt))
        V, A_, P_, T_ = nc.vector, nc.scalar, nc.gpsimd, nc.tensor
        ones = sb("ones", [128, 128]); dng = sb("dng", [128, 1]); fg = sb("fg", [128, KC])
        ld = [sb(f"ld{i}", [128, TS]) for i in range(4)]
        t1 = [sb(f"t1_{i}", [128, TS]) for i in range(2)]; rstd = sb("rstd", [128, TS])
        tdb = sb("tdb", [128, HK, TS], BF16); mgb = sb("mgb", [128, KC, TS], BF16); xn = sb("xn", [128, KC, TS])
        ot = [sb(f"ot{i}", [128, TS]) for i in range(2)]
        NWB = 3
        wb = [sb(f"wb{i}", [128, KC, 128], BF16) for i in range(NWB)]
        ps = [st.enter_context(nc.psum_tensor(f"ps{i}", [128, 512], F32)) for i in range(4)]
        for i in range(4): S.bank_of[f"ps{i}"] = i
        S.op("pool", lambda: P_.memset(ones[:], 1.0), writes=["ones"])
        S.dma("sp", "c_dng", lambda: nc.sync.dma_start(out=dng[:], in_=dngc), writes=["dng"])
        S.dma("sp", "c_fg", lambda: nc.sync.dma_start(out=fg[:], in_=fgc), writes=["fg"])
        wcount = [0]; lcount = [0]; pcount = [0]; ocount = [0]

        def load_w(wap, c0, kcn):
            i = wcount[0] % NWB; wcount[0] += 1
            for k0 in range(0, kcn, 8):
                k1 = min(kcn, k0 + 8)
                S.dma("pool", f"w{i}", lambda: P_.dma_start(out=wb[i][:, k0:k1, :], in_=wap[k0 * 128:k1 * 128, c0:c0 + 128].rearrange("(kc p) c -> p kc c", p=128)), writes=[f"wb{i}"])
            return i

        def load(src_ap):
            i = lcount[0] % 4; lcount[0] += 1
            S.dma("sp", f"ld{i}", lambda: nc.sync.dma_start(out=ld[i][:], in_=src_ap), writes=[f"ld{i}"])
            return i

        def bank():
            b = pcount[0] % 3; pcount[0] += 1; return b

        for s in range(NS):
            tsl = slice(s * TS, (s + 1) * TS)
            for hb in range(HK):
                io = load(oT[hb * 128:(hb + 1) * 128, tsl]); iz = load(szdT[hb * 128:(hb + 1) * 128, tsl]); i = hb % 2
                S.op("act", lambda: A_.activation(out=t1[i][:], in_=ld[io][:], func=AF.Square), reads=[f"ld{io}"], writes=[f"t1_{i}"])
                S.op("pe", lambda: T_.matmul(ps[3][:], ones[:], t1[i][:], start=True, stop=True), reads=["ones", f"t1_{i}"], writes=["ps3"])
                S.op("act", lambda: A_.activation(out=rstd[:], in_=ps[3][:], func=AF.Sqrt, scale=1.0 / 128, bias=EPS), reads=["ps3"], writes=["rstd"])
                S.op("dve", lambda: V.reciprocal(out=rstd[:], in_=rstd[:]), reads=["rstd"], writes=["rstd"])
                S.op("dve", lambda: V.scalar_tensor_tensor(out=t1[i][:], in0=ld[io][:], scalar=dng[:, 0:1], in1=rstd[:], op0=ALU.mult, op1=ALU.mult),
                     reads=[f"ld{io}", "dng", "rstd"], writes=[f"t1_{i}"])
                S.op("dve", lambda: V.tensor_tensor(out=tdb[:, hb, :], in0=t1[i][:], in1=ld[iz][:], op=ALU.mult), reads=[f"t1_{i}", f"ld{iz}"], writes=["tdb"])
            for ob in range(KC):
                wi = load_w(w_brd, ob * 128, HK); b = bank()
                for kc in range(HK):
                    S.op("pe", lambda: T_.matmul(ps[b][:], wb[wi][:, kc, :], tdb[:, kc, :], start=(kc == 0), stop=(kc == HK - 1)),
                         reads=[f"wb{wi}", "tdb"], writes=[f"ps{b}"], signal=(kc == HK - 1))
                ig = load(sgdT[ob * 128:(ob + 1) * 128, tsl]); i1 = load(G1T[ob * 128:(ob + 1) * 128, tsl]); i = ob % 2
                S.op("dve", lambda: V.tensor_tensor(out=t1[i][:], in0=ps[b][:], in1=ld[ig][:], op=ALU.mult), reads=[f"ps{b}", f"ld{ig}"], writes=[f"t1_{i}"])
                S.op("dve", lambda: V.tensor_tensor(out=mgb[:, ob, :], in0=t1[i][:], in1=ld[i1][:], op=ALU.add), reads=[f"t1_{i}", f"ld{i1}"], writes=["mgb"])
            for ob in range(KC):
                wi = load_w(w_out, ob * 128, KC); b = bank()
                for kc in range(KC):
                    S.op("pe", lambda: T_.matmul(ps[b][:], wb[wi][:, kc, :], mgb[:, kc, :], start=(kc == 0), stop=(kc == KC - 1)),
                         reads=[f"wb{wi}", "mgb"], writes=[f"ps{b}"], signal=(kc == KC - 1))
                ix = load(xT[ob * 128:(ob + 1) * 128, tsl])
                S.op("dve", lambda: V.tensor_tensor(out=xn[:, ob, :], in0=ps[b][:], in1=ld[ix][:], op=ALU.add), reads=[f"ps{b}", f"ld{ix}"], writes=[("xn", ob)])
                if not last:
                    S.dma("sp", f"st_xn{ob % 4}", lambda: nc.sync.dma_start(out=yT[ob * 128:(ob + 1) * 128, tsl], in_=xn[:, ob, :]), reads=[("xn", ob)], writes=[("yT", ob, s)])
                    outs.append(("yT", ob, s))
            if last:
                for kc in range(KC):
                    i = kc % 2
                    S.op("act", lambda: A_.activation(out=t1[i][:], in_=xn[:, kc, :], func=AF.Square), reads=[("xn", kc)], writes=[f"t1_{i}"])
                    S.op("pe", lambda: T_.matmul(ps[3][:], ones[:], t1[i][:], start=(kc == 0), stop=(kc == KC - 1)), reads=["ones", f"t1_{i}"], writes=["ps3"], signal=(kc == KC - 1))
                S.op("act", lambda: A_.activation(out=rstd[:], in_=ps[3][:], func=AF.Sqrt, scale=1.0 / D, bias=EPS), reads=["ps3"], writes=["rstd"])
                S.op("dve", lambda: V.reciprocal(out=rstd[:], in_=rstd[:]), reads=["rstd"], writes=["rstd"])
                for kc in range(KC):
                    o = ocount[0] % 2; ocount[0] += 1
                    S.op("dve", lambda: V.scalar_tensor_tensor(out=ot[o][:], in0=xn[:, kc, :], scalar=fg[:, kc:kc + 1], in1=rstd[:], op0=ALU.mult, op1=ALU.mult),
                         reads=[("xn", kc), "fg", "rstd"], writes=[f"ot{o}"])
                    S.dma("sp", f"st_ot{o}", lambda: nc.sync.dma_start(out=yT[kc * 128:(kc + 1) * 128, tsl], in_=ot[o][:]), reads=[f"ot{o}"], writes=[("yT", kc, s)])
                    outs.append(("yT", kc, s))
        S.drain("sp", outs)
        print("launch C: instrs", S.n_instr, "sems", S.nsem)
    return nc


def host_inputs_C(D, p, l, x_full, o_full, A_out, c, NT):
    KC = D // 128
    f = lambda a: np.ascontiguousarray(a, dtype=np.float32)
    t0 = c * NT
    return {"xT": f(x_full[t0:t0 + NT].T), "oT": f(o_full[t0:t0 + NT].T), "szdT": A_out["szdT"], "sgdT": A_out["sgdT"], "G1T": A_out["G1T"],
            "w_brd": f(p["w_br_dn"][l]), "w_out": f(p["w_out"][l]), "dngc": f(p["dn_norm_g"][l].reshape(128, 1)),
            "fgc": f(p["final_g"].reshape(KC, 128).T)}


_D, _S, _NC, _H, _DEPTH = 4096, 16384, 8, 16, 4
_NT = _S // _NC
_HB = _H // _NC
_PROGS = {}


def _prog(name, fn):
    if name not in _PROGS:
        _PROGS[name] = fn()
    return _PROGS[name]


def _run(nc, in_maps):
    return run_bass_kernel_spmd(nc, in_maps, core_ids=list(range(_NC))).results


def kernel(**inputs):
    p = {k: np.asarray(v) for k, v in inputs.items()}
    x_full = np.ascontiguousarray(p["x"][0], dtype=np.float32)
    for l in range(_DEPTH):
        ncA = _prog("A", lambda: build_A(_D, _NT))
        resA = _run(ncA, [host_inputs_A(_D, p, l, x_full, c, _NT) for c in range(_NC)])
        tok = lambda name: np.concatenate([resA[c][name].reshape(_H, 128, _NT).transpose(2, 0, 1) for c in range(_NC)], 0)
        q, k, v = tok("qT"), tok("kT"), tok("vT")
        beta = np.concatenate([resA[c]["betaT"].T for c in range(_NC)], 0)
        g = np.concatenate([resA[c]["gT"].T for c in range(_NC)], 0)
        ncB = _prog("B", lambda: build_B(_S, _HB))
        resB = _run(ncB, [host_inputs_B(q, k, v, beta, g, [c * _HB + j for j in range(_HB)]) for c in range(_NC)])
        o_full = np.concatenate([resB[c]["o"][j] for c in range(_NC) for j in range(_HB)], 1)
        del q, k, v
        last = (l == _DEPTH - 1)
        ncC = _prog("C_last" if last else "C", lambda: build_C(_D, _NT, last))
        A_keep = [{n: resA[c][n] for n in ("szdT", "sgdT", "G1T")} for c in range(_NC)]
        resC = _run(ncC, [host_inputs_C(_D, p, l, x_full, o_full, A_keep[c], c, _NT) for c in range(_NC)])
        x_full = np.ascontiguousarray(np.concatenate([resC[c]["yT"].T for c in range(_NC)], 0), dtype=np.float32)
        del resA, resB, resC, A_keep, o_full
    return x_full[None]
```

```python
import contextlib
import numpy as np
import concourse.bass as bass
import concourse.mybir as mybir
from concourse.bass_utils import run_bass_kernel_spmd

_ENV = {"LA_DBG": "0"}


F32 = mybir.dt.float32
BF16 = mybir.dt.bfloat16
AF = mybir.ActivationFunctionType
ALU = mybir.AluOpType
AX = mybir.AxisListType

SEM_ROLL = 24000


class Sync:
    def __init__(self, nc, stack):
        self.nc = nc
        self.stack = stack
        self.eng = {"pe": nc.tensor, "act": nc.scalar, "dve": nc.vector, "pool": nc.gpsimd, "sp": nc.sync}
        self.sem = {}
        self.cnt = {}
        self.nsem = 0
        self.waited = {}
        self.lastw = {}
        self.readers = {}
        self.pending = {}
        self.unsig = {}
        self.n_instr = 0
        self.bank_of = {}
        self.bank_last = {}
        self.issued = {}
        self.log = []
        self.semname = {}

    def _new_sem(self, name):
        self.nsem += 1
        s = self.stack.enter_context(self.nc.semaphore(f"s{self.nsem}_{name}"))
        self.sem[name] = s
        self.semname[s.num] = name
        self.cnt[name] = 0
        return s

    def _producer(self, name, inc):
        if name not in self.sem or self.cnt[name] + inc > SEM_ROLL:
            self._new_sem(name)
        self.cnt[name] += inc
        return (self.sem[name], self.cnt[name])

    def _wait(self, e, ev):
        if ev is None:
            return
        sem, val = ev
        if sem.num in self.issued:
            val = self.issued[sem.num]
        key = (e, sem.num)
        if self.waited.get(key, 0) >= val:
            return
        self.waited[key] = val
        self.eng[e].wait_ge(sem, val)
        self.log.append((e, 'wait', f'{self.semname[sem.num]}>={val}'))
        self.n_instr += 1

    def _flush(self, e2):
        parked = self.unsig.pop(e2, [])
        if not parked:
            return
        ev = self._producer(e2, 1)
        parked[-1][2].then_inc(ev[0], 1)
        self.log.append((e2, 'flush', f'retro-signal -> {e2}={ev[1]}'))
        for (rs, ws, _i) in parked:
            self._commit(ev, rs, ws, eng=e2)

    def _conflicts(self, e, reads, writes):
        rset, wset = set(reads), set(writes)
        banks = {self.bank_of[r] for r in rset | wset if r in self.bank_of}
        for e2 in list(self.unsig.keys()):
            if e2 == e:
                continue
            for (rs, ws, _i) in self.unsig[e2]:
                hit = (wset & (set(rs) | set(ws))) or (rset & set(ws)) or (banks & {self.bank_of[r] for r in tuple(rs) + tuple(ws) if r in self.bank_of})
                if hit:
                    self._flush(e2)
                    break

    def _deps(self, e, reads, writes):
        self._conflicts(e, reads, writes)
        for reg in list(reads) + list(writes):
            b = self.bank_of.get(reg)
            if b is not None:
                for e2, ev2 in self.bank_last.get(b, {}).items():
                    if e2 != e:
                        self._wait(e, ev2)
        for r in reads:
            self._wait(e, self.lastw.get(r))
        for w in writes:
            self._wait(e, self.lastw.get(w))
            for ev in self.readers.get(w, ()):
                self._wait(e, ev)

    def _commit(self, ev, reads, writes, eng=None):
        for reg in list(reads) + list(writes):
            b = self.bank_of.get(reg)
            if b is not None and eng is not None:
                self.bank_last.setdefault(b, {})[eng] = ev
        for r in reads:
            self.readers.setdefault(r, []).append(ev)
        for w in writes:
            self.lastw[w] = ev
            self.readers[w] = []

    def op(self, e, fn, reads=(), writes=(), signal=True):
        self._deps(e, reads, writes)
        ins = fn()
        self.n_instr += 1
        self.log.append((e, 'op', f'r={list(reads)} w={list(writes)} sig={signal}'))
        if signal:
            ev = self._producer(e, 1)
            self.log[-1] = (e, 'op', self.log[-1][2] + f' -> {e}={ev[1]}')
            ins.then_inc(ev[0], 1)
            for (rs, ws, _i) in self.unsig.pop(e, []):
                self._commit(ev, rs, ws, eng=e)
            self._commit(ev, reads, writes, eng=e)
        else:
            self.unsig.setdefault(e, []).append((tuple(reads), tuple(writes), ins))
            for w in writes:
                self.lastw[w] = ("UNSIG", e)
        return ins

    def dma(self, q, stream, fn, reads=(), writes=()):
        self._deps(q, reads, writes)
        ins = fn()
        self.n_instr += 1
        ev = self._producer("dma_" + stream, 16)
        ins.then_inc(ev[0], 16)
        self.issued[ev[0].num] = ev[1]
        self._commit(ev, reads, writes)
        return ins

    def fence(self, engines=("pe", "act", "dve", "pool")):
        for e in engines:
            for name, sem in list(self.sem.items()):
                if self.cnt[name] > 0:
                    self._wait(e, (sem, self.cnt[name]))

    def drain(self, e, regions):
        for r in regions:
            self._wait(e, self.lastw.get(r))


_orig_wait = Sync._wait


def _checked_wait(self, e, ev):
    if ev is not None and ev[0] == "UNSIG" and ev[1] == e:
        return
    if ev is not None and ev[0] == "UNSIG":
        raise RuntimeError(f"dependency on unsignalled instruction of engine {ev[1]}")
    return _orig_wait(self, e, ev)


Sync._wait = _checked_wait


EPS = 1e-6
TS = 512
FENCE_TAGS = set(filter(None, _ENV.get("FENCE_TAGS", "").split(",")))
FENCE_LN = bool(FENCE_TAGS)


def _nm(t):
    return t.name if hasattr(t, "name") else t.tensor.name


def build_A(D, NT, first_core_has_halo=True):
    KC = D // 128; GW = D // 2; GG = GW // 128; DW = D // 2; H = DW // 128
    sizes = (GW, GW, GW, 3 * DW, DW, H, H, D, D)
    off = [0]
    for s in sizes: off.append(off[-1] + s)
    NCOL = off[-1]
    NS = NT // TS
    nc = bass.Bass("TRN2", target_bir_lowering=False)
    din = lambda n, s: nc.dram_tensor(n, s, F32, kind="ExternalInput").ap()
    dout = lambda n, s: nc.dram_tensor(n, s, F32, kind="ExternalOutput").ap()
    xT = din("xT", [D, NT]); xhT = din("xhT", [D, 4]); ngc = din("ngc", [128, KC])
    w_in = din("w_in", [D, NCOL]); cw = din("cw", [128, 3 * H, 4])
    lngc = din("lngc", [128, GG]); lnbc = din("lnbc", [128, GG])
    wsT = din("wsT", [128, GG, 128]); maskT = din("maskT", [128, 128]); bsb = din("bsb", [128, GG * 128])
    w_brg = din("w_brg", [GW, D]); alc = din("alc", [H, 1]); dtc = din("dtc", [H, 1])
    G1T = dout("G1T", [D, NT]); sgdT = dout("sgdT", [D, NT]); szdT = dout("szdT", [DW, NT])
    qT = dout("qT", [DW, NT]); kT = dout("kT", [DW, NT]); vT = dout("vT", [DW, NT])
    betaT = dout("betaT", [H, NT]); gT = dout("gT", [H, NT])
    DBG = _ENV.get("LA_DBG", "1") == "1"
    DUMP_LN = DBG and _ENV.get("DUMP_LN", "0") == "1"
    if DUMP_LN:
        rawD = dout("rawD", [GG, NT // 128, 128, 128]); mx2D = dout("mx2D", [GG, NT // 128, 128, 128])
        mvD = dout("mvD", [NT // 128, 128, 2]); rsD = dout("rsD", [NT // 128, 128, 1]); yD = dout("yD", [NT // 128, 128, GW])
    if DBG:
        mixD = dout("mixD", [GW, NT]); tgD = dout("tgD", [GW, NT])
    outs_regions = []

    with contextlib.ExitStack() as st:
        S = Sync(nc, st)
        sb = lambda name, shape, dt=F32: st.enter_context(nc.sbuf_tensor(name, shape, dt))
        ones = sb("ones", [128, 128]); onesb = sb("onesb", [128, 128], BF16)
        ngt = sb("ngt", [128, KC]); cwt = sb("cwt", [128, 3 * H, 4]); lng = sb("lng", [128, GG]); lnb = sb("lnb", [128, GG])
        wsm = sb("wsm", [128, GG, 128]); wsb = sb("wsb", [128, GG, 128], BF16); mk = sb("mk", [128, 128])
        CB = sb("CB", [128, GG * 128]); nea = sb("nea", [H, 1]); dtt = sb("dtt", [H, 1])
        xs = [sb(f"xs{i}", [128, TS]) for i in range(2)]
        sq = [sb(f"sq{i}", [128, TS]) for i in range(2)]
        rstd = sb("rstd", [128, TS]); xh = sb("xh", [128, KC, 4]); sqh = sb("sqh", [128, KC, 4]); rsh = sb("rsh", [128, 4])
        hT = sb("hT", [128, KC, TS], BF16); hTh = sb("hTh", [128, KC, 4], BF16)
        NWB = 3
        wb = [sb(f"wb{i}", [128, KC, 128], BF16) for i in range(NWB)]
        vtm = sb("vtm", [128, TS // 128, GW]); YBUFS = int(_ENV.get("YBUFS", "1"))
        ybfs = [sb(f"ybf{i}", [128, GW], BF16) for i in range(YBUFS)]
        bst = sb("bst", [128, (GW + 511) // 512, 6]); mv = sb("mv", [128, 2]); rs2 = sb("rs2", [128, 1])
        NEED_VSQ = (_ENV.get("PLAIN_STATS", "0") == "1") or (DBG and DUMP_LN)
        if NEED_VSQ:
            vsq = sb("vsq", [128, GW]); msq = sb("msq", [128, 1])
        if DBG and DUMP_LN:
            rawS = sb("rawS", [128, GG, 128])
        mixT = sb("mixT", [128, GG, TS]); tgb = sb("tgb", [128, GG, TS], BF16)
        t1 = [sb(f"t1_{i}", [128, TS]) for i in range(2)]; t2 = [sb(f"t2_{i}", [128, TS]) for i in range(2)]
        ot = [sb(f"ot{i}", [128, TS]) for i in range(3)]
        cv = [sb(f"cv{i}", [128, TS + 3]) for i in range(2)]; halo = sb("halo", [128, 3 * H, 3])
        ca = [sb(f"ca{i}", [128, TS]) for i in range(2)]
        sp_t = sb("sp_t", [H, TS]); sp_u = sb("sp_u", [H, TS])
        ps = [st.enter_context(nc.psum_tensor(f"ps{i}", [128, 512], F32)) for i in range(8)]
        V, A_, P_, T_ = nc.vector, nc.scalar, nc.gpsimd, nc.tensor
        for _i in range(8): S.bank_of[f"ps{_i}"] = _i

        S.op("pool", lambda: P_.memset(ones[:], 1.0), writes=["ones"])
        S.op("pool", lambda: P_.memset(onesb[:], 1.0), writes=["onesb"])
        for (t, src, nm) in [(ngt, ngc, "ngt"), (cwt, cw, "cwt"), (lng, lngc, "lng"), (lnb, lnbc, "lnb"), (wsm, wsT, "wsm"),
                             (mk, maskT, "mk"), (CB, bsb, "CB"), (nea, alc, "nea"), (dtt, dtc, "dtt")]:
            S.dma("sp", "const", lambda: nc.sync.dma_start(out=t[:], in_=src), writes=[nm])
        for g in range(GG):
            S.op("dve", lambda: V.tensor_tensor(out=wsm[:, g, :], in0=wsm[:, g, :], in1=mk[:], op=ALU.mult),
                 reads=["wsm", "mk"], writes=["wsm"])
        S.op("dve", lambda: V.tensor_copy(out=wsb[:], in_=wsm[:]), reads=["wsm"], writes=["wsb"])
        for g4 in range(0, GG, 4):
            n = min(4, GG - g4)
            S.op("pe", lambda: T_.matmul(ps[4][:, 0:n * 128], ones[:], wsm[:, g4:g4 + n, :].rearrange("p g c -> p (g c)"),
                                         start=True, stop=True), reads=["ones", "wsm"], writes=["ps4"])
            for g in range(g4, g4 + n):
                S.op("dve", lambda: V.scalar_tensor_tensor(out=CB[:, g * 128:(g + 1) * 128], in0=ps[4][:, (g - g4) * 128:(g - g4 + 1) * 128],
                                                           scalar=lnb[:, g:g + 1], in1=CB[:, g * 128:(g + 1) * 128],
                                                           op0=ALU.mult, op1=ALU.add), reads=["ps4", "lnb", "CB"], writes=["CB"])
        S.op("act", lambda: A_.activation(out=nea[:], in_=nea[:], func=AF.Exp), reads=["nea"], writes=["nea"])
        S.op("dve", lambda: V.tensor_scalar(out=nea[:], in0=nea[:], scalar1=-1.0, scalar2=None, op0=ALU.mult), reads=["nea"], writes=["nea"])

        wcount = [0]

        def load_w(wap, c0, ncols, kcn):
            i = wcount[0] % NWB; wcount[0] += 1
            step = 8
            for k0 in range(0, kcn, step):
                k1 = min(kcn, k0 + step)
                S.dma("pool", f"w{i}", lambda: P_.dma_start(
                    out=wb[i][:, k0:k1, 0:ncols],
                    in_=wap[k0 * 128:k1 * 128, c0:c0 + ncols].rearrange("(kc p) c -> p kc c", p=128)),
                    writes=[f"wb{i}"])
            return i

        def proj(bank, wi, ncols, kcn, act, ntok=TS, t0=0):
            for kc in range(kcn):
                S.op("pe", lambda: T_.matmul(ps[bank][0:ncols, 0:ntok], wb[wi][:, kc, 0:ncols], act[:, kc, t0:t0 + ntok],
                                             start=(kc == 0), stop=(kc == kcn - 1)),
                     reads=[f"wb{wi}", "act_" + _nm(act)], writes=[f"ps{bank}"], signal=(kc == kcn - 1))

        pb = [0]
        def nextbank():
            b = pb[0] % 3; pb[0] += 1; return b
        oc = [0]
        def nextot():
            i = oc[0] % 3; oc[0] += 1; return i

        def store(dst_ap, src_tile_ap, src_region, dst_region):
            S.dma("sp", f"st_{src_region}", lambda: nc.sync.dma_start(out=dst_ap, in_=src_tile_ap), reads=[src_region], writes=[dst_region])
            outs_regions.append(dst_region)

        def rmsnorm_to(dst, src_dram, ntok, xs_, sq_, rstd_):
            for kc in range(KC):
                b = kc % 2
                S.dma("sp", f"x{b}", lambda: nc.sync.dma_start(out=xs_[b][:, 0:ntok], in_=src_dram[kc * 128:(kc + 1) * 128, :]), writes=[f"xs{b}"])
                S.op("act", lambda: A_.activation(out=sq_[b][:, 0:ntok], in_=xs_[b][:, 0:ntok], func=AF.Square), reads=[f"xs{b}"], writes=[f"sq{b}"])
                S.op("pe", lambda: T_.matmul(ps[3][:, 0:ntok], ones[:], sq_[b][:, 0:ntok], start=(kc == 0), stop=(kc == KC - 1)),
                     reads=["ones", f"sq{b}"], writes=["ps3"], signal=(kc == KC - 1))
            S.op("act", lambda: A_.activation(out=rstd_[:, 0:ntok], in_=ps[3][:, 0:ntok], func=AF.Sqrt, scale=1.0 / D, bias=EPS), reads=["ps3"], writes=["rstd"])
            S.op("dve", lambda: V.reciprocal(out=rstd_[:, 0:ntok], in_=rstd_[:, 0:ntok]), reads=["rstd"], writes=["rstd"])
            for kc in range(KC):
                b = kc % 2
                S.dma("sp", f"x{b}", lambda: nc.sync.dma_start(out=xs_[b][:, 0:ntok], in_=src_dram[kc * 128:(kc + 1) * 128, :]), writes=[f"xs{b}"])
                S.op("dve", lambda: V.scalar_tensor_tensor(out=dst[:, kc, 0:ntok], in0=xs_[b][:, 0:ntok], scalar=ngt[:, kc:kc + 1], in1=rstd_[:, 0:ntok],
                                                           op0=ALU.mult, op1=ALU.mult), reads=[f"xs{b}", "ngt", "rstd"], writes=["act_" + _nm(dst)])

        rmsnorm_to(hTh, xhT, 4, xs, sq, rstd)

        for s in range(NS):
            tsl = slice(s * TS, (s + 1) * TS)
            rmsnorm_to(hT, xT[:, tsl], TS, xs, sq, rstd)

            for vb in range(GG):
                wi = load_w(w_in, off[1] + vb * 128, 128, KC)
                for tb in range(TS // 128):
                    bank = nextbank()
                    for kc in range(KC):
                        S.op("pe", lambda: T_.matmul(ps[bank][:, 0:128], hT[:, kc, tb * 128:(tb + 1) * 128], wb[wi][:, kc, :],
                                                     start=(kc == 0), stop=(kc == KC - 1)),
                             reads=[f"wb{wi}", "act_hT"], writes=[f"ps{bank}"], signal=(kc == KC - 1))
                    S.op("act", lambda: A_.activation(out=vtm[:, tb, vb * 128:(vb + 1) * 128], in_=ps[bank][:, 0:128], func=AF.Gelu),
                         reads=[f"ps{bank}"], writes=[("vtm", tb)])
            _op = S.op
            S.op = lambda e, fn, tag=None, **kw: _op(e, fn, **kw)
            if FENCE_LN:
                def _fenced(e, fn, tag=None, **kw):
                    if tag in FENCE_TAGS or "all" in FENCE_TAGS: S.fence()
                    return _op(e, fn, **kw)
                S.op = _fenced
            for tb in range(TS // 128):
                ybf = ybfs[tb % YBUFS]; YR = f"ybf{tb % YBUFS}"
                nchunk = (GW + 511) // 512
                if _ENV.get("PLAIN_STATS", "0") == "1":
                    S.op("dve", lambda: V.reduce_sum(out=mv[:, 0:1], in_=vtm[:, tb, :], axis=AX.X), reads=[("vtm", tb)], writes=["mv"])
                    S.op("dve", lambda: V.tensor_tensor(out=vsq[:], in0=vtm[:, tb, :], in1=vtm[:, tb, :], op=ALU.mult), reads=[("vtm", tb)], writes=["vsq"])
                    S.op("dve", lambda: V.reduce_sum(out=mv[:, 1:2], in_=vsq[:], axis=AX.X), reads=["vsq", "mv"], writes=["mv"])
                    S.op("dve", lambda: V.tensor_scalar(out=mv[:], in0=mv[:], scalar1=1.0 / GW, scalar2=None, op0=ALU.mult), reads=["mv"], writes=["mv"])
                    S.op("dve", lambda: V.tensor_tensor(out=msq[:], in0=mv[:, 0:1], in1=mv[:, 0:1], op=ALU.mult), reads=["mv"], writes=["msq"])
                    S.op("dve", lambda: V.tensor_tensor(out=mv[:, 1:2], in0=mv[:, 1:2], in1=msq[:], op=ALU.subtract), reads=["mv", "msq"], writes=["mv"])
                else:
                    for c in range(nchunk):
                        S.op("dve", tag="stats", fn=lambda: V.bn_stats(out=bst[:, c, :], in_=vtm[:, tb, c * 512:min(GW, (c + 1) * 512)]), reads=[("vtm", tb)], writes=["bst"])
                    S.op("dve", tag="aggr", fn=lambda: V.bn_aggr(out=mv[:], in_=bst[:].rearrange("p c s -> p (c s)")), reads=["bst"], writes=["mv"])
                S.op("act", tag="sqrt", fn=lambda: A_.activation(out=rs2[:], in_=mv[:, 1:2], func=AF.Sqrt, bias=EPS), reads=["mv"], writes=["rs2"])
                S.op("dve", tag="recip", fn=lambda: V.reciprocal(out=rs2[:], in_=rs2[:]), reads=["rs2"], writes=["rs2"])
                S.op("dve", tag="norm", fn=lambda: V.tensor_scalar(out=ybf[:], in0=vtm[:, tb, :], scalar1=mv[:, 0:1], scalar2=rs2[:, 0:1], op0=ALU.subtract, op1=ALU.mult),
                     reads=[("vtm", tb), "mv", "rs2"], writes=[YR])
                if DUMP_LN:
                    gtb = s * (TS // 128) + tb
                    store(mvD[gtb], mv[:], "mv", ("mvD", gtb))
                    store(rsD[gtb], rs2[:], "rs2", ("rsD", gtb))
                    S.op("dve", lambda: V.tensor_copy(out=vsq[:], in_=ybf[:]), reads=[YR], writes=["vsq"])
                    store(yD[gtb], vsq[:], "vsq", ("yD", gtb))
                for g4 in range(0, GG, 4):
                    n = min(4, GG - g4); bank = 4 + (g4 // 4) % 2
                    PE_DRAIN = _ENV.get("PE_DRAIN", "0") == "1"
                    for g in range(g4, g4 + n):
                        S.op("pe", tag="mm", fn=lambda: T_.matmul(ps[bank][:, (g - g4) * 128:(g - g4 + 1) * 128], ybf[:, g * 128:(g + 1) * 128], wsb[:, g, :], start=True, stop=True),
                             reads=[YR, "wsb"], writes=[f"ps{bank}"], signal=(g == g4 + n - 1) and not PE_DRAIN)
                    if PE_DRAIN:
                        S.op("pe", tag="mm", fn=lambda: T_.matmul(ps[6][:, 16:32], onesb[:, 0:128], onesb[:, 0:16], start=True, stop=True),
                             reads=["onesb"], writes=["ps6"], signal=True)
                    for g in range(g4, g4 + n):
                        if DUMP_LN:
                            gtb2 = s * (TS // 128) + tb
                            S.op("dve", lambda: V.tensor_copy(out=rawS[:, g, :], in_=ps[bank][:, (g - g4) * 128:(g - g4 + 1) * 128]), reads=[f"ps{bank}"], writes=[("rawS", g)])
                            store(rawD[g, gtb2], rawS[:, g, :], ("rawS", g), ("rawD", g, gtb2))
                        S.op("dve", tag="evac", fn=lambda: V.scalar_tensor_tensor(out=mixT[:, g, tb * 128:(tb + 1) * 128], in0=ps[bank][:, (g - g4) * 128:(g - g4 + 1) * 128],
                                                                   scalar=lng[:, g:g + 1], in1=CB[:, g * 128:(g + 1) * 128], op0=ALU.mult, op1=ALU.add),
                             reads=[f"ps{bank}", "lng", "CB"], writes=[("mixT", g)])
                        if DUMP_LN:
                            store(mx2D[g, gtb2], mixT[:, g, tb * 128:(tb + 1) * 128], ("mixT", g), ("mx2D", g, gtb2))

            S.op = _op
            if "all" in FENCE_TAGS: S.fence()
            for g in range(GG):
                wu = load_w(w_in, off[0] + g * 128, 128, KC); bu = nextbank(); proj(bu, wu, 128, KC, hT)
                wz = load_w(w_in, off[2] + g * 128, 128, KC); bz = nextbank(); proj(bz, wz, 128, KC, hT)
                i = g % 2
                S.op("act", lambda: A_.activation(out=t1[i][:], in_=ps[bu][:], func=AF.Gelu), reads=[f"ps{bu}"], writes=[f"t1_{i}"])
                S.op("act", lambda: A_.activation(out=t2[i][:], in_=ps[bz][:], func=AF.Silu), reads=[f"ps{bz}"], writes=[f"t2_{i}"])
                S.op("dve", lambda: V.tensor_tensor(out=t1[i][:], in0=t1[i][:], in1=mixT[:, g, :], op=ALU.mult), reads=[f"t1_{i}", ("mixT", g)], writes=[f"t1_{i}"])
                S.op("dve", lambda: V.tensor_tensor(out=tgb[:, g, :], in0=t1[i][:], in1=t2[i][:], op=ALU.mult), reads=[f"t1_{i}", f"t2_{i}"], writes=["act_tgb"])

            if DBG:
                tgf = ot
                for g in range(GG):
                    store(mixD[g * 128:(g + 1) * 128, tsl], mixT[:, g, :], ("mixT", g), ("mixD", g, s))
                    o = nextot()
                    S.op("dve", lambda: V.tensor_copy(out=ot[o][:], in_=tgb[:, g, :]), reads=["act_tgb"], writes=[f"ot{o}"])
                    store(tgD[g * 128:(g + 1) * 128, tsl], ot[o][:], f"ot{o}", ("tgD", g, s))
            for ob in range(KC):
                wy = load_w(w_brg, ob * 128, 128, GG)
                for kc in range(GG):
                    S.op("pe", lambda: T_.matmul(ps[7][:], wb[wy][:, kc, :], tgb[:, kc, :], start=(kc == 0), stop=(kc == GG - 1)),
                         reads=[f"wb{wy}", "act_tgb"], writes=["ps7"], signal=(kc == GG - 1))
                wg = load_w(w_in, off[7] + ob * 128, 128, KC); bg = nextbank(); proj(bg, wg, 128, KC, hT)
                i = ob % 2; o = nextot()
                S.op("act", lambda: A_.activation(out=t2[i][:], in_=ps[bg][:], func=AF.Sigmoid), reads=[f"ps{bg}"], writes=[f"t2_{i}"])
                S.op("dve", lambda: V.tensor_tensor(out=ot[o][:], in0=ps[7][:], in1=t2[i][:], op=ALU.mult), reads=["ps7", f"t2_{i}"], writes=[f"ot{o}"])
                store(G1T[ob * 128:(ob + 1) * 128, tsl], ot[o][:], f"ot{o}", ("G1T", ob, s))

            for cb in range(3 * H):
                wq = load_w(w_in, off[3] + cb * 128, 128, KC); bq = nextbank(); proj(bq, wq, 128, KC, hT)
                i = cb % 2
                if s == 0:
                    for kc in range(KC):
                        S.op("pe", lambda: T_.matmul(ps[6][:, 0:4], wb[wq][:, kc, :], hTh[:, kc, :], start=(kc == 0), stop=(kc == KC - 1)),
                             reads=[f"wb{wq}", "act_hTh"], writes=["ps6"], signal=(kc == KC - 1))
                    S.op("dve", lambda: V.tensor_copy(out=cv[i][:, 0:3], in_=ps[6][:, 0:3]), reads=["ps6"], writes=[f"cv{i}"])
                else:
                    S.op("dve", lambda: V.tensor_copy(out=cv[i][:, 0:3], in_=halo[:, cb, :]), reads=[("halo", cb)], writes=[f"cv{i}"])
                S.op("act", lambda: A_.copy(out=cv[i][:, 3:TS + 3], in_=ps[bq][:]), reads=[f"ps{bq}"], writes=[f"cv{i}"])
                S.op("dve", lambda: V.tensor_copy(out=halo[:, cb, :], in_=cv[i][:, TS:TS + 3]), reads=[f"cv{i}"], writes=[("halo", cb)])
                S.op("dve", lambda: V.tensor_scalar(out=ca[i][:], in0=cv[i][:, 0:TS], scalar1=cwt[:, cb, 0:1], scalar2=None, op0=ALU.mult),
                     reads=[f"cv{i}", "cwt"], writes=[f"ca{i}"])
                for j in range(1, 4):
                    S.op("dve", lambda: V.scalar_tensor_tensor(out=ca[i][:], in0=cv[i][:, j:j + TS], scalar=cwt[:, cb, j:j + 1], in1=ca[i][:], op0=ALU.mult, op1=ALU.add),
                         reads=[f"cv{i}", "cwt", f"ca{i}"], writes=[f"ca{i}"])
                o = nextot()
                kind = cb // H
                if kind == 2:
                    S.op("act", lambda: A_.activation(out=ot[o][:], in_=ca[i][:], func=AF.Silu), reads=[f"ca{i}"], writes=[f"ot{o}"])
                    store(vT[(cb - 2 * H) * 128:(cb - 2 * H + 1) * 128, tsl], ot[o][:], f"ot{o}", ("vT", cb, s))
                else:
                    S.op("act", lambda: A_.activation(out=ca[i][:], in_=ca[i][:], func=AF.Silu), reads=[f"ca{i}"], writes=[f"ca{i}"])
                    S.op("act", lambda: A_.activation(out=t1[i][:], in_=ca[i][:], func=AF.Square), reads=[f"ca{i}"], writes=[f"t1_{i}"])
                    S.op("pe", lambda: T_.matmul(ps[3][:], ones[:], t1[i][:], start=True, stop=True), reads=["ones", f"t1_{i}"], writes=["ps3"])
                    S.op("act", lambda: A_.activation(out=t1[i][:], in_=ps[3][:], func=AF.Sqrt, bias=EPS), reads=["ps3"], writes=[f"t1_{i}"])
                    S.op("dve", lambda: V.reciprocal(out=t1[i][:], in_=t1[i][:]), reads=[f"t1_{i}"], writes=[f"t1_{i}"])
                    sc = (128 ** -0.5) if kind == 0 else 1.0
                    S.op("dve", lambda: V.scalar_tensor_tensor(out=ot[o][:], in0=ca[i][:], scalar=sc, in1=t1[i][:], op0=ALU.mult, op1=ALU.mult),
                         reads=[f"ca{i}", f"t1_{i}"], writes=[f"ot{o}"])
                    dst = qT if kind == 0 else kT
                    hb = cb - kind * H
                    store(dst[hb * 128:(hb + 1) * 128, tsl], ot[o][:], f"ot{o}", ("qk", cb, s))

            for zb in range(DW // 128):
                wz = load_w(w_in, off[4] + zb * 128, 128, KC); bz = nextbank(); proj(bz, wz, 128, KC, hT); o = nextot()
                S.op("act", lambda: A_.activation(out=ot[o][:], in_=ps[bz][:], func=AF.Silu), reads=[f"ps{bz}"], writes=[f"ot{o}"])
                store(szdT[zb * 128:(zb + 1) * 128, tsl], ot[o][:], f"ot{o}", ("szd", zb, s))
            for gb in range(KC):
                wg = load_w(w_in, off[8] + gb * 128, 128, KC); bg = nextbank(); proj(bg, wg, 128, KC, hT); o = nextot()
                S.op("act", lambda: A_.activation(out=ot[o][:], in_=ps[bg][:], func=AF.Sigmoid), reads=[f"ps{bg}"], writes=[f"ot{o}"])
                store(sgdT[gb * 128:(gb + 1) * 128, tsl], ot[o][:], f"ot{o}", ("sgd", gb, s))

            wbt = load_w(w_in, off[5], H, KC); bb = nextbank(); proj(bb, wbt, H, KC, hT); o = nextot()
            S.op("act", lambda: A_.activation(out=ot[o][0:H, :], in_=ps[bb][0:H, :], func=AF.Sigmoid), reads=[f"ps{bb}"], writes=[f"ot{o}"])
            store(betaT[:, tsl], ot[o][0:H, :], f"ot{o}", ("beta", s))
            wa = load_w(w_in, off[6], H, KC); ba = nextbank(); proj(ba, wa, H, KC, hT); o = nextot()
            S.op("dve", lambda: V.tensor_scalar(out=sp_t[:], in0=ps[ba][0:H, :], scalar1=dtt[:, 0:1], scalar2=None, op0=ALU.add), reads=[f"ps{ba}", "dtt"], writes=["sp_t"])
            S.op("act", lambda: A_.activation(out=sp_u[:], in_=sp_t[:], func=AF.Abs), reads=["sp_t"], writes=["sp_u"])
            S.op("act", lambda: A_.activation(out=sp_u[:], in_=sp_u[:], func=AF.Exp, scale=-1.0), reads=["sp_u"], writes=["sp_u"])
            S.op("act", lambda: A_.activation(out=sp_u[:], in_=sp_u[:], func=AF.Ln, bias=1.0), reads=["sp_u"], writes=["sp_u"])
            S.op("dve", lambda: V.scalar_tensor_tensor(out=sp_t[:], in0=sp_t[:], scalar=0.0, in1=sp_u[:], op0=ALU.max, op1=ALU.add), reads=["sp_t", "sp_u"], writes=["sp_t"])
            S.op("dve", lambda: V.tensor_scalar(out=ot[o][0:H, :], in0=sp_t[:], scalar1=nea[:, 0:1], scalar2=None, op0=ALU.mult), reads=["sp_t", "nea"], writes=[f"ot{o}"])
            store(gT[:, tsl], ot[o][0:H, :], f"ot{o}", ("g", s))

        S.drain("sp", outs_regions)
        print("launch A: instrs", S.n_instr, "sems", S.nsem)
    return nc


def host_inputs_A(cfg_D, p, l, x_full, c, NT):
    D = cfg_D; KC = D // 128; GW = D // 2; GG = GW // 128; H = GW // 128
    f = lambda a: np.ascontiguousarray(a, dtype=np.float32)
    t0 = c * NT
    xh = np.zeros((4, D), np.float32)
    if c > 0: xh[0:3] = x_full[t0 - 3:t0]
    cid = np.arange(128) // 64
    return {
        "xT": f(x_full[t0:t0 + NT].T), "xhT": f(xh.T), "ngc": f(p["norm_g"][l].reshape(KC, 128).T),
        "w_in": f(p["w_in"][l]), "cw": f(p["conv_w"][l].T.reshape(3 * H, 128, 4).transpose(1, 0, 2)),
        "lngc": f(p["ln_g"][l].reshape(GG, 128).T), "lnbc": f(p["ln_b"][l].reshape(GG, 128).T),
        "wsT": f(p["w_s"][l].transpose(2, 0, 1)), "maskT": f((cid[:, None] <= cid[None, :])),
        "bsb": f(np.broadcast_to(p["b_s"][l].reshape(1, GG * 128), (128, GG * 128))),
        "w_brg": f(p["w_br_gmlp"][l]), "alc": f(p["a_log"][l].reshape(H, 1)), "dtc": f(p["dt_bias"][l].reshape(H, 1)),
    }


C = 128
STOP = int(_ENV.get('STOP_B', '99'))
SL = 8


def build_B(S_len, HB):
    NCH = S_len // C
    nc = bass.Bass("TRN2", target_bir_lowering=False)
    din = lambda n, s: nc.dram_tensor(n, s, F32, kind="ExternalInput").ap()
    qT = din("qT", [HB, 128, S_len]); kT = din("kT", [HB, 128, S_len]); vT = din("vT", [HB, 128, S_len])
    bM = din("bM", [HB, 128, NCH]); gM = din("gM", [HB, 128, NCH])
    ktm = din("ktm", [HB, S_len, 128]); vtm = din("vtm", [HB, S_len, 128])
    cst = din("cst", [7, 128, 128])
    o = nc.dram_tensor("o", [HB, S_len, 128], F32, kind="ExternalOutput").ap()
    outs = []
    with contextlib.ExitStack() as st:
        Sy = Sync(nc, st)
        sb = lambda name, shape, dt=F32: st.enter_context(nc.sbuf_tensor(name, shape, dt))
        V, A_, T_ = nc.vector, nc.scalar, nc.tensor
        ident = sb("ident", [128, 128]); ones = sb("ones", [128, 128]); UT = sb("UT", [128, 128])
        mLs = sb("mLs", [128, 128]); mLi = sb("mLi", [128, 128]); El = sb("El", [128, 128]); mUs = sb("mUs", [128, 128])
        for i, (t, nm) in enumerate([(ident, "ident"), (ones, "ones"), (UT, "UT"), (mLs, "mLs"), (mLi, "mLi"), (El, "El"), (mUs, "mUs")]):
            Sy.dma("sp", f"c{i}", lambda: nc.sync.dma_start(out=t[:], in_=cst[i]), writes=[nm])
        banks = [st.enter_context(nc.psum_tensor(f"pb{i}", [128, 512], F32)) for i in range(8)]
        slot_i = [0]
        def pslot():
            i = slot_i[0] % 32; slot_i[0] += 1
            b, q = i % 8, i // 8
            Sy.bank_of[f"pslot{i}"] = b
            return banks[b][:, q * 128:(q + 1) * 128], f"pslot{i}"

        per = {}
        for h in range(HB):
            P = {}
            for nm in ("beta", "gcum", "gam", "kdsc", "geB", "bgam", "nbeta", "glB", "graw"):
                P[nm] = sb(f"{nm}{h}", [128, NCH])
            P["S"] = [sb(f"S{h}_{i}", [128, 128]) for i in range(2)]
            P["q"] = [sb(f"q{h}_{i}", [128, SL * C]) for i in range(2)]
            P["k"] = [sb(f"k{h}_{i}", [128, SL * C]) for i in range(2)]
            P["v"] = [sb(f"v{h}_{i}", [128, SL * C]) for i in range(2)]
            P["kt"] = [sb(f"kt{h}_{i}", [128, SL, 128]) for i in range(2)]
            P["vt"] = [sb(f"vt{h}_{i}", [128, SL, 128]) for i in range(2)]
            for nm in ("bgk", "kd", "bv", "diagG", "tA", "decS", "tB", "decB", "AqkT", "kcT", "usb", "w", "awsb", "osb", "diagNB", "decBs", "ytmp"):
                P[nm] = sb(f"{nm}{h}", [128, 128])
            P["X"] = [sb(f"X{h}_{i}", [128, 128]) for i in range(2)]
            P["Y"] = [sb(f"Y{h}_{i}", [128, 128]) for i in range(2)]
            P["P"] = [sb(f"P{h}_{i}", [128, 128]) for i in range(2)]
            per[h] = P
            R = lambda nm: f"{nm}{h}"
            Sy.dma("sp", f"bg{h}", lambda: nc.sync.dma_start(out=P["beta"][:], in_=bM[h]), writes=[R("beta")])
            Sy.dma("sp", f"bg{h}", lambda: nc.sync.dma_start(out=P["graw"][:], in_=gM[h]), writes=[R("graw")])
            ps, pr = pslot()
            Sy.op("pe", lambda: T_.matmul(ps[:, 0:NCH], UT[:], P["graw"][:], start=True, stop=True), reads=["UT", R("graw")], writes=[pr])
            Sy.op("act", lambda: A_.copy(out=P["gcum"][:], in_=ps[:, 0:NCH]), reads=[pr], writes=[R("gcum")])
            ps2, pr2 = pslot()
            Sy.op("pe", lambda: T_.matmul(ps2[:, 0:NCH], El[:], P["gcum"][:], start=True, stop=True), reads=["El", R("gcum")], writes=[pr2])
            Sy.op("act", lambda: A_.copy(out=P["glB"][:], in_=ps2[:, 0:NCH]), reads=[pr2], writes=[R("glB")])
            Sy.op("act", lambda: A_.activation(out=P["gam"][:], in_=P["gcum"][:], func=AF.Exp), reads=[R("gcum")], writes=[R("gam")])
            Sy.op("act", lambda: A_.activation(out=P["geB"][:], in_=P["glB"][:], func=AF.Exp), reads=[R("glB")], writes=[R("geB")])
            Sy.op("dve", lambda: V.tensor_tensor(out=P["kdsc"][:], in0=P["glB"][:], in1=P["gcum"][:], op=ALU.subtract), reads=[R("glB"), R("gcum")], writes=[R("kdsc")])
            Sy.op("act", lambda: A_.activation(out=P["kdsc"][:], in_=P["kdsc"][:], func=AF.Exp), reads=[R("kdsc")], writes=[R("kdsc")])
            Sy.op("dve", lambda: V.tensor_tensor(out=P["bgam"][:], in0=P["beta"][:], in1=P["gam"][:], op=ALU.mult), reads=[R("beta"), R("gam")], writes=[R("bgam")])
            Sy.op("dve", lambda: V.tensor_scalar(out=P["nbeta"][:], in0=P["beta"][:], scalar1=-1.0, scalar2=None, op0=ALU.mult), reads=[R("beta")], writes=[R("nbeta")])
            Sy.op("dve", lambda: V.memset(P["S"][0][:], 0.0), writes=[R("S0")])

        def chunk_gen(n, h):
            P = per[h]; R = lambda nm: f"{nm}{h}"
            sl = (n // SL) % 2; c0 = (n % SL) * C
            if n % SL == 0:
                t0 = n * C; t1 = min(S_len, t0 + SL * C)
                for nm, src in (("q", qT), ("k", kT), ("v", vT)):
                    Sy.dma("sp", f"{nm}{h}_{sl}", lambda: nc.sync.dma_start(out=P[nm][sl][:, 0:t1 - t0], in_=src[h][:, t0:t1]), writes=[R(f"{nm}sl{sl}")])
                    yield
                nsl = (t1 - t0) // C
                for nm, src in (("kt", ktm), ("vt", vtm)):
                    Sy.dma("sp", f"{nm}{h}_{sl}", lambda: nc.sync.dma_start(out=P[nm][sl][:, 0:nsl, :], in_=src[h][t0:t1, :].rearrange("(s c) d -> c s d", c=C)), writes=[R(f"{nm}sl{sl}")])
                    yield
            qc = P["q"][sl][:, c0:c0 + C]; kc = P["k"][sl][:, c0:c0 + C]; vc = P["v"][sl][:, c0:c0 + C]
            Rq, Rk, Rv = R(f"qsl{sl}"), R(f"ksl{sl}"), R(f"vsl{sl}")
            col = lambda nm: P[nm][:, n:n + 1]
            ktc = P["kt"][sl][:, n % SL, :]; vtc = P["vt"][sl][:, n % SL, :]; Rkt, Rvt = R(f"ktsl{sl}"), R(f"vtsl{sl}")
            Sy.op("dve", lambda: V.tensor_scalar(out=P["bgk"][:], in0=ktc, scalar1=col("bgam"), scalar2=None, op0=ALU.mult), reads=[Rkt, R("bgam")], writes=[R("bgk")])
            yield
            Sy.op("dve", lambda: V.tensor_scalar(out=P["kd"][:], in0=ktc, scalar1=col("kdsc"), scalar2=None, op0=ALU.mult), reads=[Rkt, R("kdsc")], writes=[R("kd")])
            yield
            Sy.op("dve", lambda: V.tensor_scalar(out=P["bv"][:], in0=vtc, scalar1=col("beta"), scalar2=None, op0=ALU.mult), reads=[Rvt, R("beta")], writes=[R("bv")])
            yield
            if STOP <= 1: return
            Sy.op("dve", lambda: V.tensor_scalar(out=P["diagG"][:], in0=ident[:], scalar1=col("gcum"), scalar2=None, op0=ALU.mult), reads=["ident", R("gcum")], writes=[R("diagG")])
            yield
            Sy.op("dve", lambda: V.tensor_scalar(out=P["diagNB"][:], in0=ident[:], scalar1=col("nbeta"), scalar2=None, op0=ALU.mult), reads=["ident", R("nbeta")], writes=[R("diagNB")])
            yield
            pR, rR = pslot(); pKK, rKK = pslot(); pQK, rQK = pslot(); pRB, rRB = pslot()
            Sy.op("pe", lambda: T_.matmul(pRB, ones[:], P["diagNB"][:], start=True, stop=True), reads=["ones", R("diagNB")], writes=[rRB])
            yield
            Sy.op("pe", lambda: T_.matmul(pR, ones[:], P["diagG"][:], start=True, stop=True), reads=["ones", R("diagG")], writes=[rR])
            yield
            Sy.op("pe", lambda: T_.matmul(pKK, kc, kc, start=True, stop=True), reads=[Rk], writes=[rKK])
            yield
            Sy.op("pe", lambda: T_.matmul(pQK, kc, qc, start=True, stop=True), reads=[Rk, Rq], writes=[rQK])
            yield
            if STOP <= 2: return
            Sy.op("dve", lambda: V.tensor_scalar(out=P["ytmp"][:], in0=pR, scalar1=col("gcum"), scalar2=None, op0=ALU.subtract), reads=[rR, R("gcum")], writes=[R("ytmp")])
            yield
            Sy.op("dve", lambda: V.tensor_scalar_max(out=P["tA"][:], in0=P["ytmp"][:], scalar1=0.0), reads=[R("ytmp")], writes=[R("tA")])
            yield
            Sy.op("dve", lambda: V.tensor_scalar_min(out=P["tB"][:], in0=P["ytmp"][:], scalar1=0.0), reads=[R("ytmp")], writes=[R("tB")])
            yield
            Sy.op("act", lambda: A_.activation(out=P["tA"][:], in_=P["tA"][:], func=AF.Exp, scale=-1.0), reads=[R("tA")], writes=[R("tA")])
            yield
            Sy.op("dve", lambda: V.tensor_tensor(out=P["decS"][:], in0=P["tA"][:], in1=mLs[:], op=ALU.mult), reads=[R("tA"), "mLs"], writes=[R("decS")])
            yield
            Sy.op("act", lambda: A_.activation(out=P["tB"][:], in_=P["tB"][:], func=AF.Exp), reads=[R("tB")], writes=[R("tB")])
            yield
            Sy.op("dve", lambda: V.tensor_tensor(out=P["decB"][:], in0=P["tB"][:], in1=UT[:], op=ALU.mult), reads=[R("tB"), "UT"], writes=[R("decB")])
            yield
            Sy.op("dve", lambda: V.tensor_tensor(out=P["decBs"][:], in0=P["tB"][:], in1=mUs[:], op=ALU.mult), reads=[R("tB"), "mUs"], writes=[R("decBs")])
            yield
            Sy.op("dve", lambda: V.scalar_tensor_tensor(out=P["X"][0][:], in0=pKK, scalar=col("nbeta"), in1=P["decS"][:], op0=ALU.mult, op1=ALU.mult),
                  reads=[rKK, R("nbeta"), R("decS")], writes=[R("X0")])
            yield
            Sy.op("dve", lambda: V.tensor_tensor(out=P["AqkT"][:], in0=pQK, in1=P["decB"][:], op=ALU.mult), reads=[rQK, R("decB")], writes=[R("AqkT")])
            yield
            if STOP <= 3: return
            Sy.op("dve", lambda: V.tensor_tensor(out=P["ytmp"][:], in0=pKK, in1=P["decBs"][:], op=ALU.mult), reads=[rKK, R("decBs")], writes=[R("ytmp")])
            yield
            Sy.op("dve", lambda: V.tensor_tensor(out=P["Y"][0][:], in0=pRB, in1=P["ytmp"][:], op=ALU.mult), reads=[rRB, R("ytmp")], writes=[R("Y0")])
            yield
            Sy.op("dve", lambda: V.tensor_tensor(out=P["P"][0][:], in0=P["Y"][0][:], in1=ident[:], op=ALU.add), reads=[R("Y0"), "ident"], writes=[R("P0")])
            yield
            NLEV = 6
            for m in range(NLEV):
                a, b = m % 2, (m + 1) % 2
                pX, rX = pslot()
                Sy.op("pe", lambda: T_.matmul(pX, P["Y"][a][:], P["X"][a][:], start=True, stop=True), reads=[R(f"Y{a}"), R(f"X{a}")], writes=[rX])
                yield
                if m < NLEV - 1:
                    pY2, rY2 = pslot()
                    Sy.op("pe", lambda: T_.matmul(pY2, P["X"][a][:], P["Y"][a][:], start=True, stop=True), reads=[R(f"X{a}"), R(f"Y{a}")], writes=[rY2])
                    yield
                Sy.op("act", lambda: A_.copy(out=P["X"][b][:], in_=pX), reads=[rX], writes=[R(f"X{b}")])
                yield
                if m < NLEV - 1:
                    Sy.op("dve", lambda: V.tensor_copy(out=P["Y"][b][:], in_=pY2), reads=[rY2], writes=[R(f"Y{b}")])
                    yield
                pP, rP = pslot()
                Sy.op("pe", lambda: T_.matmul(pP, P["X"][b][:], P["P"][a][:], start=True, stop=True), reads=[R(f"X{b}"), R(f"P{a}")], writes=[rP])
                yield
                Sy.op("dve", lambda: V.tensor_tensor(out=P["P"][b][:], in0=pP, in1=P["P"][a][:], op=ALU.add), reads=[rP, R(f"P{a}")], writes=[R(f"P{b}")])
                yield
            Mi = P["P"][NLEV % 2]; RMi = R(f"P{NLEV % 2}")
            if STOP <= 4: return
            pKc, rKc = pslot(); pU, rU = pslot()
            Sy.op("pe", lambda: T_.matmul(pKc, P["bgk"][:], Mi[:], start=True, stop=True), reads=[R("bgk"), RMi], writes=[rKc])
            yield
            Sy.op("pe", lambda: T_.matmul(pU, Mi[:], P["bv"][:], start=True, stop=True), reads=[RMi, R("bv")], writes=[rU])
            yield
            Sy.op("act", lambda: A_.copy(out=P["kcT"][:], in_=pKc), reads=[rKc], writes=[R("kcT")])
            yield
            Sy.op("act", lambda: A_.copy(out=P["usb"][:], in_=pU), reads=[rU], writes=[R("usb")])
            yield
            if STOP <= 5: return
            sa, sbn = n % 2, (n + 1) % 2
            Scur = P["S"][sa]; RS = R(f"S{sa}")
            pW, rW = pslot(); pQS, rQS = pslot()
            Sy.op("pe", lambda: T_.matmul(pW, P["kcT"][:], Scur[:], start=True, stop=True), reads=[R("kcT"), RS], writes=[rW])
            yield
            Sy.op("pe", lambda: T_.matmul(pQS, qc, Scur[:], start=True, stop=True), reads=[Rq, RS], writes=[rQS])
            yield
            Sy.op("dve", lambda: V.scalar_tensor_tensor(out=P["w"][:], in0=pW, scalar=-1.0, in1=P["usb"][:], op0=ALU.mult, op1=ALU.add), reads=[R("usb"), rW], writes=[R("w")])
            yield
            pAW, rAW = pslot(); pS, rS = pslot()
            Sy.op("pe", lambda: T_.matmul(pAW, P["AqkT"][:], P["w"][:], start=True, stop=True), reads=[R("AqkT"), R("w")], writes=[rAW])
            yield
            Sy.op("pe", lambda: T_.matmul(pS, P["kd"][:], P["w"][:], start=True, stop=True), reads=[R("kd"), R("w")], writes=[rS])
            yield
            Sy.op("act", lambda: A_.copy(out=P["awsb"][:], in_=pAW), reads=[rAW], writes=[R("awsb")])
            yield
            Sy.op("dve", lambda: V.scalar_tensor_tensor(out=P["osb"][:], in0=pQS, scalar=col("gam"), in1=P["awsb"][:], op0=ALU.mult, op1=ALU.add),
                  reads=[rQS, R("gam"), R("awsb")], writes=[R("osb")])
            yield
            Sy.dma("sp", f"o{h}", lambda: nc.sync.dma_start(out=o[h][n * C:(n + 1) * C, :], in_=P["osb"][:]), reads=[R("osb")], writes=[("o", h, n)])
            yield
            outs.append(("o", h, n))
            Sy.op("dve", lambda: V.tensor_scalar(out=P["ytmp"][:], in0=Scur[:], scalar1=col("geB"), scalar2=None, op0=ALU.mult), reads=[RS, R("geB")], writes=[R("ytmp")])
            yield
            Sy.op("dve", lambda: V.tensor_tensor(out=P["S"][sbn][:], in0=pS, in1=P["ytmp"][:], op=ALU.add), reads=[rS, R("ytmp")], writes=[R(f"S{sbn}")])
            yield

        for n in range(NCH if STOP > 0 else 0):
            alive = [chunk_gen(n, h) for h in range(HB)]
            while alive:
                for gch in list(alive):
                    try:
                        next(gch)
                    except StopIteration:
                        alive.remove(gch)
        Sy.drain("sp", outs)
        print("launch B: instrs", Sy.n_instr, "sems", Sy.nsem)
    return nc


def consts_B():
    i = np.arange(128)
    ident = np.eye(128); ones = np.ones((128, 128)); UT = (i[:, None] <= i[None, :])
    mLs = (i[None, :] < i[:, None]); mLi = (i[None, :] <= i[:, None]); El = np.zeros((128, 128)); El[127, :] = 1
    mUs = (i[:, None] < i[None, :])
    return np.ascontiguousarray(np.stack([ident, ones, UT, mLs, mLi, El, mUs]).astype(np.float32))


def host_inputs_B(q, k, v, beta, g, heads):
    f = lambda a: np.ascontiguousarray(a, dtype=np.float32)
    S_len = q.shape[0]; NCH = S_len // C
    return {"qT": f(np.stack([q[:, h].T for h in heads])), "kT": f(np.stack([k[:, h].T for h in heads])),
            "vT": f(np.stack([v[:, h].T for h in heads])),
            "ktm": f(np.stack([k[:, h] for h in heads])), "vtm": f(np.stack([v[:, h] for h in heads])),
            "bM": f(np.stack([beta[:, h].reshape(NCH, C).T for h in heads])),
            "gM": f(np.stack([g[:, h].reshape(NCH, C).T for h in heads])), "cst": consts_B()}


EPS = 1e-6
TS = 512


def build_C(D, NT, last):
    KC = D // 128; DW = D // 2; HK = DW // 128
    NS = NT // TS
    nc = bass.Bass("TRN2", target_bir_lowering=False)
    din = lambda n, s: nc.dram_tensor(n, s, F32, kind="ExternalInput").ap()
    xT = din("xT", [D, NT]); oT = din("oT", [DW, NT]); szdT = din("szdT", [DW, NT]); sgdT = din("sgdT", [D, NT]); G1T = din("G1T", [D, NT])
    w_brd = din("w_brd", [DW, D]); w_out = din("w_out", [D, D]); dngc = din("dngc", [128, 1]); fgc = din("fgc", [128, KC])
    yT = nc.dram_tensor("yT", [D, NT], F32, kind="ExternalOutput").ap()
    outs = []
    with contextlib.ExitStack() as st:
        S = Sync(nc, st)
        sb = lambda name, shape, dt=F32: st.enter_context(nc.sbuf_tensor(name, shape, dt))
        V, A_, P_, T_ = nc.vector, nc.scalar, nc.gpsimd, nc.tensor
        ones = sb("ones", [128, 128]); dng = sb("dng", [128, 1]); fg = sb("fg", [128, KC])
        ld = [sb(f"ld{i}", [128, TS]) for i in range(4)]
        t1 = [sb(f"t1_{i}", [128, TS]) for i in range(2)]; rstd = sb("rstd", [128, TS])
        tdb = sb("tdb", [128, HK, TS], BF16); mgb = sb("mgb", [128, KC, TS], BF16); xn = sb("xn", [128, KC, TS])
        ot = [sb(f"ot{i}", [128, TS]) for i in range(2)]
        NWB = 3
        wb = [sb(f"wb{i}", [128, KC, 128], BF16) for i in range(NWB)]
        ps = [st.enter_context(nc.psum_tensor(f"ps{i}", [128, 512], F32)) for i in range(4)]
        for i in range(4): S.bank_of[f"ps{i}"] = i
        S.op("pool", lambda: P_.memset(ones[:], 1.0), writes=["ones"])
        S.dma("sp", "c_dng", lambda: nc.sync.dma_start(out=dng[:], in_=dngc), writes=["dng"])
        S.dma("sp", "c_fg", lambda: nc.sync.dma_start(out=fg[:], in_=fgc), writes=["fg"])
        wcount = [0]; lcount = [0]; pcount = [0]; ocount = [0]

        def load_w(wap, c0, kcn):
            i = wcount[0] % NWB; wcount[0] += 1
            for k0 in range(0, kcn, 8):
                k1 = min(kcn, k0 + 8)
                S.dma("pool", f"w{i}", lambda: P_.dma_start(out=wb[i][:, k0:k1, :], in_=wap[k0 * 128:k1 * 128, c0:c0 + 128].rearrange("(kc p) c -> p kc c", p=128)), writes=[f"wb{i}"])
            return i

        def load(src_ap):
            i = lcount[0] % 4; lcount[0] += 1
            S.dma("sp", f"ld{i}", lambda: nc.sync.dma_start(out=ld[i][:], in_=src_ap), writes=[f"ld{i}"])
            return i

        def bank():
            b = pcount[0] % 3; pcount[0] += 1; return b

        for s in range(NS):
            tsl = slice(s * TS, (s + 1) * TS)
            for hb in range(HK):
                io = load(oT[hb * 128:(hb + 1) * 128, tsl]); iz = load(szdT[hb * 128:(hb + 1) * 128, tsl]); i = hb % 2
                S.op("act", lambda: A_.activation(out=t1[i][:], in_=ld[io][:], func=AF.Square), reads=[f"ld{io}"], writes=[f"t1_{i}"])
                S.op("pe", lambda: T_.matmul(ps[3][:], ones[:], t1[i][:], start=True, stop=True), reads=["ones", f"t1_{i}"], writes=["ps3"])
                S.op("act", lambda: A_.activation(out=rstd[:], in_=ps[3][:], func=AF.Sqrt, scale=1.0 / 128, bias=EPS), reads=["ps3"], writes=["rstd"])
                S.op("dve", lambda: V.reciprocal(out=rstd[:], in_=rstd[:]), reads=["rstd"], writes=["rstd"])
                S.op("dve", lambda: V.scalar_tensor_tensor(out=t1[i][:], in0=ld[io][:], scalar=dng[:, 0:1], in1=rstd[:], op0=ALU.mult, op1=ALU.mult),
                     reads=[f"ld{io}", "dng", "rstd"], writes=[f"t1_{i}"])
                S.op("dve", lambda: V.tensor_tensor(out=tdb[:, hb, :], in0=t1[i][:], in1=ld[iz][:], op=ALU.mult), reads=[f"t1_{i}", f"ld{iz}"], writes=["tdb"])
            for ob in range(KC):
                wi = load_w(w_brd, ob * 128, HK); b = bank()
                for kc in range(HK):
                    S.op("pe", lambda: T_.matmul(ps[b][:], wb[wi][:, kc, :], tdb[:, kc, :], start=(kc == 0), stop=(kc == HK - 1)),
                         reads=[f"wb{wi}", "tdb"], writes=[f"ps{b}"], signal=(kc == HK - 1))
                ig = load(sgdT[ob * 128:(ob + 1) * 128, tsl]); i1 = load(G1T[ob * 128:(ob + 1) * 128, tsl]); i = ob % 2
                S.op("dve", lambda: V.tensor_tensor(out=t1[i][:], in0=ps[b][:], in1=ld[ig][:], op=ALU.mult), reads=[f"ps{b}", f"ld{ig}"], writes=[f"t1_{i}"])
                S.op("dve", lambda: V.tensor_tensor(out=mgb[:, ob, :], in0=t1[i][:], in1=ld[i1][:], op=ALU.add), reads=[f"t1_{i}", f"ld{i1}"], writes=["mgb"])
            for ob in range(KC):
                wi = load_w(w_out, ob * 128, KC); b = bank()
                for kc in range(KC):
                    S.op("pe", lambda: T_.matmul(ps[b][:], wb[wi][:, kc, :], mgb[:, kc, :], start=(kc == 0), stop=(kc == KC - 1)),
                         reads=[f"wb{wi}", "mgb"], writes=[f"ps{b}"], signal=(kc == KC - 1))
                ix = load(xT[ob * 128:(ob + 1) * 128, tsl])
                S.op("dve", lambda: V.tensor_tensor(out=xn[:, ob, :], in0=ps[b][:], in1=ld[ix][:], op=ALU.add), reads=[f"ps{b}", f"ld{ix}"], writes=[("xn", ob)])
                if not last:
                    S.dma("sp", f"st_xn{ob % 4}", lambda: nc.sync.dma_start(out=yT[ob * 128:(ob + 1) * 128, tsl], in_=xn[:, ob, :]), reads=[("xn", ob)], writes=[("yT", ob, s)])
                    outs.append(("yT", ob, s))
            if last:
                for kc in range(KC):
                    i = kc % 2
                    S.op("act", lambda: A_.activation(out=t1[i][:], in_=xn[:, kc, :], func=AF.Square), reads=[("xn", kc)], writes=[f"t1_{i}"])
                    S.op("pe", lambda: T_.matmul(ps[3][:], ones[:], t1[i][:], start=(kc == 0), stop=(kc == KC - 1)), reads=["ones", f"t1_{i}"], writes=["ps3"], signal=(kc == KC - 1))
                S.op("act", lambda: A_.activation(out=rstd[:], in_=ps[3][:], func=AF.Sqrt, scale=1.0 / D, bias=EPS), reads=["ps3"], writes=["rstd"])
                S.op("dve", lambda: V.reciprocal(out=rstd[:], in_=rstd[:]), reads=["rstd"], writes=["rstd"])
                for kc in range(KC):
                    o = ocount[0] % 2; ocount[0] += 1
                    S.op("dve", lambda: V.scalar_tensor_tensor(out=ot[o][:], in0=xn[:, kc, :], scalar=fg[:, kc:kc + 1], in1=rstd[:], op0=ALU.mult, op1=ALU.mult),
                         reads=[("xn", kc), "fg", "rstd"], writes=[f"ot{o}"])
                    S.dma("sp", f"st_ot{o}", lambda: nc.sync.dma_start(out=yT[kc * 128:(kc + 1) * 128, tsl], in_=ot[o][:]), reads=[f"ot{o}"], writes=[("yT", kc, s)])
                    outs.append(("yT", kc, s))
        S.drain("sp", outs)
        print("launch C: instrs", S.n_instr, "sems", S.nsem)
    return nc


def host_inputs_C(D, p, l, x_full, o_full, A_out, c, NT):
    KC = D // 128
    f = lambda a: np.ascontiguousarray(a, dtype=np.float32)
    t0 = c * NT
    return {"xT": f(x_full[t0:t0 + NT].T), "oT": f(o_full[t0:t0 + NT].T), "szdT": A_out["szdT"], "sgdT": A_out["sgdT"], "G1T": A_out["G1T"],
            "w_brd": f(p["w_br_dn"][l]), "w_out": f(p["w_out"][l]), "dngc": f(p["dn_norm_g"][l].reshape(128, 1)),
            "fgc": f(p["final_g"].reshape(KC, 128).T)}


_D, _S, _NC, _H, _DEPTH = 4096, 16384, 8, 16, 4
_NT = _S // _NC
_HB = _H // _NC
_PROGS = {}


def _prog(name, fn):
    if name not in _PROGS:
        _PROGS[name] = fn()
    return _PROGS[name]


def _run(nc, in_maps):
    return run_bass_kernel_spmd(nc, in_maps, core_ids=list(range(_NC))).results


def kernel(**inputs):
    p = {k: np.asarray(v) for k, v in inputs.items()}
    x_full = np.ascontiguousarray(p["x"][0], dtype=np.float32)
    for l in range(_DEPTH):
        ncA = _prog("A", lambda: build_A(_D, _NT))
        resA = _run(ncA, [host_inputs_A(_D, p, l, x_full, c, _NT) for c in range(_NC)])
        tok = lambda name: np.concatenate([resA[c][name].reshape(_H, 128, _NT).transpose(2, 0, 1) for c in range(_NC)], 0)
        q, k, v = tok("qT"), tok("kT"), tok("vT")
        beta = np.concatenate([resA[c]["betaT"].T for c in range(_NC)], 0)
        g = np.concatenate([resA[c]["gT"].T for c in range(_NC)], 0)
        ncB = _prog("B", lambda: build_B(_S, _HB))
        resB = _run(ncB, [host_inputs_B(q, k, v, beta, g, [c * _HB + j for j in range(_HB)]) for c in range(_NC)])
        o_full = np.concatenate([resB[c]["o"][j] for c in range(_NC) for j in range(_HB)], 1)
        del q, k, v
        last = (l == _DEPTH - 1)
        ncC = _prog("C_last" if last else "C", lambda: build_C(_D, _NT, last))
        A_keep = [{n: resA[c][n] for n in ("szdT", "sgdT", "G1T")} for c in range(_NC)]
        resC = _run(ncC, [host_inputs_C(_D, p, l, x_full, o_full, A_keep[c], c, _NT) for c in range(_NC)])
        x_full = np.ascontiguousarray(np.concatenate([resC[c]["yT"].T for c in range(_NC)], 0), dtype=np.float32)
        del resA, resB, resC, A_keep, o_full
    return x_full[None]
```
